# Optimizing a Trainium2 kernel written in Bass

```python
import jax, jax.numpy as jnp
from jax import lax
import numpy as np

D_MODEL = 1024
BATCH = 8
SEQ = 4096
DEPTH = 4
DEC_BATCH = 2
DEC_SEQ = 16384
PAST_LEN = 128

N_MIXERS = 2
D_FF = 2816
CONV_WIDTH = 3
N_HEADS = 8
D_QK = D_MODEL // 2
D_V = D_MODEL
DH_QK = D_QK // N_HEADS
DH_V = D_V // N_HEADS
CHUNK = 64
FORGET_BIAS = 3.0
EPS = 1e-6

kernel_name = "hybrid_shortconv_mlstm_macaron_encoder"


def rmsnorm(x, g):
    xf = x.astype(jnp.float32)
    y = xf * lax.rsqrt(jnp.mean(xf * xf, axis=-1, keepdims=True) + EPS)
    return (y * g.astype(jnp.float32)).astype(x.dtype)


def swiglu(x, w_gate, w_up, w_down):
    return (jax.nn.silu(x @ w_gate) * (x @ w_up)) @ w_down


def short_conv_mixer(x, w_in, w_dw, w_out):
    s = x.shape[1]
    b_gate, c_gate, h = jnp.split(x @ w_in, 3, axis=-1)
    z = c_gate * h
    zp = jnp.pad(z, ((0, 0), (1, 1), (0, 0)))
    y = w_dw[0] * zp[:, :s] + w_dw[1] * zp[:, 1:s + 1] + w_dw[2] * zp[:, 2:]
    return (b_gate * y) @ w_out


def mlstm_chunkwise(q, k, v, i_pre, f_pre):
    bsz, nh, s, dk = q.shape
    dv = v.shape[-1]
    nc = s // CHUNK
    q = q.reshape(bsz, nh, nc, CHUNK, dk)
    k = k.reshape(bsz, nh, nc, CHUNK, dk)
    v = v.reshape(bsz, nh, nc, CHUNK, dv)
    log_f = jax.nn.log_sigmoid(f_pre).reshape(bsz, nh, nc, CHUNK)
    log_i = i_pre.reshape(bsz, nh, nc, CHUNK)
    b = jnp.cumsum(log_f, axis=-1)
    g = b[..., -1]

    a = g[..., None] - b + log_i
    m_loc = jnp.max(a, axis=-1)
    w_loc = jnp.exp(a - m_loc[..., None])
    c_loc = jnp.einsum('bhcl,bhclk,bhclv->bhckv', w_loc, k, v)
    n_loc = jnp.einsum('bhcl,bhclk->bhck', w_loc, k)

    def step(carry, inp):
        c, n, m = carry
        g_c, m_l, c_l, n_l = inp
        m_new = jnp.maximum(g_c + m, m_l)
        s_old = jnp.exp(g_c + m - m_new)
        s_loc = jnp.exp(m_l - m_new)
        c_new = s_old[..., None, None] * c + s_loc[..., None, None] * c_l
        n_new = s_old[..., None] * n + s_loc[..., None] * n_l
        return (c_new, n_new, m_new), (c, n, m)

    init = (jnp.zeros((bsz, nh, dk, dv), jnp.float32),
            jnp.zeros((bsz, nh, dk), jnp.float32),
            jnp.zeros((bsz, nh), jnp.float32))
    xs = (jnp.moveaxis(g, 2, 0), jnp.moveaxis(m_loc, 2, 0),
          jnp.moveaxis(c_loc, 2, 0), jnp.moveaxis(n_loc, 2, 0))
    _, (c_prev, n_prev, m_prev) = lax.scan(step, init, xs)
    c_prev = jnp.moveaxis(c_prev, 0, 2)
    n_prev = jnp.moveaxis(n_prev, 0, 2)
    m_prev = jnp.moveaxis(m_prev, 0, 2)

    d = b[..., :, None] - b[..., None, :] + log_i[..., None, :]
    seen = jnp.tril(jnp.ones((CHUNK, CHUNK), dtype=bool))
    d = jnp.where(seen, d, -jnp.inf)
    m_inter = b + m_prev[..., None]
    m_out = jnp.maximum(m_inter, jnp.max(d, axis=-1))
    sc = jnp.einsum('bhctk,bhcjk->bhctj', q, k) * jnp.exp(d - m_out[..., None])
    w_inter = jnp.exp(m_inter - m_out)
    num = (jnp.einsum('bhctj,bhcjv->bhctv', sc, v)
           + w_inter[..., None] * jnp.einsum('bhctk,bhckv->bhctv', q, c_prev))
    den = jnp.sum(sc, axis=-1) + w_inter * jnp.einsum('bhctk,bhck->bhct', q, n_prev)
    h = num / jnp.maximum(jnp.abs(den), jnp.exp(-m_out))[..., None]
    return h.reshape(bsz, nh, s, dv)


def mlstm_mixer(x, w_in, w_gate, b_gate, g_head, w_out):
    bsz, s, _ = x.shape
    q, k, v, o = jnp.split(x @ w_in, [D_QK, 2 * D_QK, 2 * D_QK + D_V], axis=-1)

    def heads(t, dh):
        return t.reshape(bsz, s, N_HEADS, dh).transpose(0, 2, 1, 3).astype(jnp.float32)

    q = heads(q, DH_QK)
    k = heads(k, DH_QK) * (DH_QK ** -0.5)
    v = heads(v, DH_V)
    gates = x.astype(jnp.float32) @ w_gate.astype(jnp.float32) + b_gate.astype(jnp.float32)
    gates = gates.reshape(bsz, s, 4, N_HEADS).transpose(2, 0, 3, 1)
    i_fw, f_fw, i_bw, f_bw = gates[0], gates[1], gates[2], gates[3]

    h_fw = mlstm_chunkwise(q, k, v, i_fw, f_fw)
    flip = lambda t: jnp.flip(t, axis=2)
    h_bw = flip(mlstm_chunkwise(flip(q), flip(k), flip(v), flip(i_bw), flip(f_bw)))
    h = h_fw + h_bw
    h = h * lax.rsqrt(jnp.mean(h * h, axis=-1, keepdims=True) + EPS)
    h = h.transpose(0, 2, 1, 3).reshape(bsz, s, D_V) * g_head.astype(jnp.float32)
    h = (h * jax.nn.sigmoid(o.astype(jnp.float32))).astype(x.dtype)
    return h @ w_out


def trunk(x, g_ffn1, w_ffn1_gate, w_ffn1_up, w_ffn1_down, g_mix,
          w_conv_in, w_conv_dw, w_conv_out,
          w_mlstm_in, w_mlstm_gate, b_mlstm_gate, g_mlstm_head, w_mlstm_out,
          g_ffn2, w_ffn2_gate, w_ffn2_up, w_ffn2_down, g_final):
    for layer in range(DEPTH):
        j = layer // N_MIXERS
        x = x + 0.5 * swiglu(rmsnorm(x, g_ffn1[layer]), w_ffn1_gate[layer],
                             w_ffn1_up[layer], w_ffn1_down[layer])
        h = rmsnorm(x, g_mix[layer])
        if layer % N_MIXERS == 0:
            x = x + short_conv_mixer(h, w_conv_in[j], w_conv_dw[j], w_conv_out[j])
        else:
            x = x + mlstm_mixer(h, w_mlstm_in[j], w_mlstm_gate[j], b_mlstm_gate[j],
                                g_mlstm_head[j], w_mlstm_out[j])
        x = x + 0.5 * swiglu(rmsnorm(x, g_ffn2[layer]), w_ffn2_gate[layer],
                             w_ffn2_up[layer], w_ffn2_down[layer])
    return rmsnorm(x, g_final)


def setup_inputs(seed: int = 0) -> dict:
    key = jax.random.key(seed)
    ks = jax.random.split(key, 24)
    n_conv = (DEPTH + 1) // 2
    n_ml = DEPTH // 2
    d = D_MODEL
    nrm = lambda k, shape, fan_in: jax.random.normal(k, shape, jnp.float32) * (fan_in ** -0.5)
    gain = lambda k, shape: 1.0 + 0.05 * jax.random.normal(k, shape, jnp.float32)
    gate_base = jnp.tile(jnp.repeat(jnp.array([0.0, FORGET_BIAS], jnp.float32), N_HEADS), 2)
    return {
        "x_prompt": jax.random.normal(ks[0], (BATCH, SEQ, d), jnp.float32),
        "x_sample": jax.random.normal(ks[1], (DEC_BATCH, DEC_SEQ, d), jnp.float32),
        "g_ffn1": gain(ks[2], (DEPTH, d)),
        "w_ffn1_gate": nrm(ks[3], (DEPTH, d, D_FF), d),
        "w_ffn1_up": nrm(ks[4], (DEPTH, d, D_FF), d),
        "w_ffn1_down": nrm(ks[5], (DEPTH, D_FF, d), D_FF),
        "g_mix": gain(ks[6], (DEPTH, d)),
        "w_conv_in": nrm(ks[7], (n_conv, d, 3 * d), d),
        "w_conv_dw": nrm(ks[8], (n_conv, CONV_WIDTH, d), CONV_WIDTH),
        "w_conv_out": nrm(ks[9], (n_conv, d, d), d),
        "w_mlstm_in": nrm(ks[10], (n_ml, d, 2 * D_QK + 2 * D_V), d),
        "w_mlstm_gate": nrm(ks[11], (n_ml, d, 4 * N_HEADS), d),
        "b_mlstm_gate": gate_base + 0.1 * jax.random.normal(ks[12], (n_ml, 4 * N_HEADS), jnp.float32),
        "g_mlstm_head": gain(ks[13], (n_ml, D_V)),
        "w_mlstm_out": nrm(ks[14], (n_ml, D_V, d), D_V),
        "g_ffn2": gain(ks[15], (DEPTH, d)),
        "w_ffn2_gate": nrm(ks[16], (DEPTH, d, D_FF), d),
        "w_ffn2_up": nrm(ks[17], (DEPTH, d, D_FF), d),
        "w_ffn2_down": nrm(ks[18], (DEPTH, D_FF, d), D_FF),
        "g_final": gain(ks[19], (d,)),
    }


def reference(x_prompt, x_sample, g_ffn1, w_ffn1_gate, w_ffn1_up, w_ffn1_down, g_mix,
              w_conv_in, w_conv_dw, w_conv_out,
              w_mlstm_in, w_mlstm_gate, b_mlstm_gate, g_mlstm_head, w_mlstm_out,
              g_ffn2, w_ffn2_gate, w_ffn2_up, w_ffn2_down, g_final):
    y_prompt = trunk(x_prompt, g_ffn1, w_ffn1_gate, w_ffn1_up, w_ffn1_down, g_mix,
                     w_conv_in, w_conv_dw, w_conv_out,
                     w_mlstm_in, w_mlstm_gate, b_mlstm_gate, g_mlstm_head, w_mlstm_out,
                     g_ffn2, w_ffn2_gate, w_ffn2_up, w_ffn2_down, g_final)
    y_sample = trunk(x_sample, g_ffn1, w_ffn1_gate, w_ffn1_up, w_ffn1_down, g_mix,
                     w_conv_in, w_conv_dw, w_conv_out,
                     w_mlstm_in, w_mlstm_gate, b_mlstm_gate, g_mlstm_head, w_mlstm_out,
                     g_ffn2, w_ffn2_gate, w_ffn2_up, w_ffn2_down, g_final)
    return (y_prompt, y_sample)
```

```python
from contextlib import ExitStack
from concourse.bass_utils import run_bass_kernel_spmd
import numpy as np
import concourse.bass as bass
import concourse.mybir as mybir

F32 = mybir.dt.float32
BF16 = mybir.dt.bfloat16
AF = mybir.ActivationFunctionType
ALU = mybir.AluOpType

COMPUTE = ("pe", "act", "dve", "pool")


class Op:
    __slots__ = ("eng", "fn", "reads", "writes", "dma", "idx", "sig", "waits", "lane", "lane_prev")

    def __init__(self, eng, fn, reads, writes, dma):
        self.eng = eng
        self.fn = fn
        self.reads = reads
        self.writes = writes
        self.dma = dma
        self.sig = None
        self.waits = None
        self.lane = None


class Prog:
    def __init__(self, nc, n_lanes=6, same_engine_sync=True):
        self.nc = nc
        self.ops = []
        self.n_lanes = n_lanes
        self.same_engine_sync = same_engine_sync

    def barrier(self):
        self.ops.append(None)

    def op(self, eng, name, kw, reads=(), writes=(), dma=False):
        o = Op(eng, (name, kw), tuple(reads), tuple(writes), dma)
        self.ops.append(o)
        return o

    def analyze(self):
        raw = self.ops
        ops = []
        bar_at = []
        for o in raw:
            if o is None:
                bar_at.append(len(ops))
            else:
                ops.append(o)
        self.ops = ops
        bar_set = set(bar_at)
        last_of = {}
        pending_bar = {}
        q_n0 = {}
        last_w = {}
        readers = {}
        deps_all = []
        needed = set()
        for i, o in enumerate(ops):
            o.idx = i
            if i in bar_set:
                alld = list(last_of.values())
                for e in ("pe", "act", "dve", "pool", "sp"):
                    pending_bar[e] = list(alld)
            deps = set()
            if pending_bar.get(o.eng):
                deps.update(pending_bar[o.eng])
                pending_bar[o.eng] = None
            if o.dma:
                k0 = q_n0.get(o.eng, 0)
                q_n0[o.eng] = k0 + 1
                last_of[("lane", o.eng, k0 % self.n_lanes)] = i
            else:
                last_of[o.eng] = i
            for r in o.reads:
                d = last_w.get(r)
                if d is not None:
                    deps.add(d)
                if type(r) is tuple and r[0] == "ps":
                    for rd in readers.get(r, ()):
                        if ops[rd].eng != o.eng:
                            deps.add(rd)
            for w in o.writes:
                d = last_w.get(w)
                if d is not None:
                    deps.add(d)
                rs = readers.get(w)
                if rs:
                    deps.update(rs)
            deps.discard(i)
            bar_deps = ()
            fd = []
            for d in deps:
                po = ops[d]
                if po.eng == o.eng and not po.dma and not o.dma:
                    if o.eng == "pe" or not self.same_engine_sync or d in bar_deps:
                        continue
                fd.append(d)
                needed.add(d)
            deps_all.append(fd)
            for r in o.reads:
                readers.setdefault(r, []).append(i)
            for w in o.writes:
                last_w[w] = i
                readers[w] = []
        cnt = {e: 0 for e in COMPUTE}
        lane_cnt = {}
        q_n = {}
        for i, o in enumerate(ops):
            if o.dma:
                k = q_n.get(o.eng, 0)
                q_n[o.eng] = k + 1
                lane = (o.eng, k % self.n_lanes)
                o.lane = lane
                c = lane_cnt.get(lane, 0)
                o.lane_prev = 16 * c
                lane_cnt[lane] = c + 1
                o.sig = (("lane",) + lane, 16 * (c + 1))
            elif i in needed:
                cnt[o.eng] += 1
                o.sig = (("eng", o.eng), cnt[o.eng])
        waited = {}
        for i, o in enumerate(ops):
            w = {}
            for d in deps_all[i]:
                s, v = ops[d].sig
                if w.get(s, 0) < v:
                    w[s] = v
            if o.dma and o.lane_prev > 0:
                s = ("lane",) + o.lane
                if w.get(s, 0) < o.lane_prev:
                    w[s] = o.lane_prev
            fw = []
            for s, v in w.items():
                key = (o.eng, s)
                if waited.get(key, 0) >= v:
                    continue
                waited[key] = v
                fw.append((s, v))
            o.waits = fw
        self.final_counts = (cnt, lane_cnt)

    def emit(self, stack):
        nc = self.nc
        self.analyze()
        sems = {}
        for e in COMPUTE:
            sems[("eng", e)] = stack.enter_context(nc.semaphore("s_" + e))
        for q in ("sp", "act", "pool"):
            for l in range(self.n_lanes):
                sems[("lane", q, l)] = stack.enter_context(nc.semaphore("l_%s%d" % (q, l)))
        per = {e: [] for e in ("pe", "act", "dve", "pool", "sp")}
        for o in self.ops:
            per[o.eng].append(o)
        block = stack.enter_context(nc.Block())
        cnt, lane_cnt = self.final_counts

        def run(eng_name, eng):
            for o in per[eng_name]:
                for s, v in o.waits:
                    eng.wait_ge(sems[s], v)
                ins = getattr(eng, o.fn[0])(**o.fn[1])
                if o.sig is not None:
                    s, v = o.sig
                    ins.then_inc(sems[s], 16 if o.dma else 1)
            for (q, l), c in lane_cnt.items():
                if q == eng_name:
                    eng.wait_ge(sems[("lane", q, l)], 16 * c)

        @block.tensor
        def _(e):
            run("pe", e)

        @block.scalar
        def _(e):
            run("act", e)

        @block.vector
        def _(e):
            run("dve", e)

        @block.gpsimd
        def _(e):
            run("pool", e)

        @block.sync
        def _(e):
            run("sp", e)


D = 1024; DFF = 2816; NT = 1024; KC = 8; FC = 22; NH = 8
EPS = 1e-6
GU_STAGES = [(0, 6), (6, 6), (12, 5), (17, 5)]
SLOT = 2 * 8 * 768
CH = 128
NEG = -30000.0
CORE_STOP = 9

def gall_layout(depth):
    off = {}
    o = 0
    for nm, n in (("ffn1", depth * 8), ("mix", depth * 8), ("ffn2", depth * 8), ("final", 8),
                  ("head", (depth // 2) * 8), ("dw", ((depth + 1) // 2) * 3 * 8)):
        off[nm] = o
        o += n
    off["_n"] = o
    return off


def build(NTOK, SEG, DEPTH, debug_stop=None):
    ntiles = NTOK // NT
    nchunks = NTOK // CH
    n_conv = (DEPTH + 1) // 2
    n_ml = DEPTH // 2
    GO = gall_layout(DEPTH)
    nc = bass.Bass("TRN2", target_bir_lowering=False)
    dt_in = lambda name, shape, dt=F32: nc.dram_tensor(name, shape, dt, kind="ExternalInput").ap()
    xT = dt_in("xT", [D, NTOK])
    w1g = dt_in("w_ffn1_gate", [DEPTH, D, DFF]); w1u = dt_in("w_ffn1_up", [DEPTH, D, DFF]); w1d = dt_in("w_ffn1_down", [DEPTH, DFF, D])
    w2g = dt_in("w_ffn2_gate", [DEPTH, D, DFF]); w2u = dt_in("w_ffn2_up", [DEPTH, D, DFF]); w2d = dt_in("w_ffn2_down", [DEPTH, DFF, D])
    wci = dt_in("w_conv_in", [n_conv, D, 3 * D]); wco = dt_in("w_conv_out", [n_conv, D, D])
    wmi = dt_in("w_mlstm_in", [max(n_ml, 1), D, 3 * D]); wmo = dt_in("w_mlstm_out", [max(n_ml, 1), D, D])
    wmg = dt_in("w_mlstm_gate", [max(n_ml, 1), D, 32])
    bmg = dt_in("b_mlstm_gate", [128, max(n_ml, 1) * 32])
    gall_d = dt_in("gall", [128, GO["_n"]])
    cm_d = dt_in("cm", [128, 1])
    cst_d = dt_in("cst", [128, 5 * 128])
    yT = nc.dram_tensor("yT", [D, NTOK], F32, kind="ExternalOutput").ap()
    xs = nc.dram_tensor("xs", [D, NTOK], F32).ap()
    zs = nc.dram_tensor("zs", [D, NTOK + 2], F32).ap()
    bs = nc.dram_tensor("bs", [D, NTOK], F32).ap()
    q_s = nc.dram_tensor("q_s", [512, NTOK], F32).ap()
    kT_s = nc.dram_tensor("kT_s", [512, NTOK], BF16).ap()
    ktm_s = nc.dram_tensor("ktm_s", [NTOK, 512], F32).ap()
    vtm_s = nc.dram_tensor("vtm_s", [NTOK, D], BF16).ap()
    o_s = nc.dram_tensor("o_s", [D, NTOK], F32).ap()
    g_s = nc.dram_tensor("g_s", [NTOK, 32], F32).ap()
    hfw_s = nc.dram_tensor("hfw_s", [D, NTOK], F32).ap()
    hT_s = nc.dram_tensor("hT_s", [D, NTOK], BF16).ap()

    st = ExitStack()
    with st:
        sb = lambda name, shape, dt: st.enter_context(nc.sbuf_tensor("sb_" + name, shape, dt))
        x = sb("x", [128, KC, NT], F32)
        xn = sb("xn", [128, KC, NT], BF16)
        arena = sb("arena", [128, FC * NT + 3 * SLOT], BF16)
        h = arena[:, 0:FC * NT].rearrange("p (k t) -> p k t", k=FC)
        wp = [arena[:, FC * NT + i * SLOT: FC * NT + (i + 1) * SLOT] for i in range(3)]
        hstage32 = arena[:, 0:FC * NT].bitcast(F32)
        rstd = sb("rstd", [128, NT], F32)
        sg = [sb("sg%d" % i, [128, 512], F32) for i in range(2)]
        ev32 = [sb("ev32_%d" % i, [128, 512], F32) for i in range(4)]
        ev16 = [sb("ev16_%d" % i, [128, 512], BF16) for i in range(4)]
        gall = sb("gall", [128, GO["_n"]], F32)
        cm = sb("cm", [128, 1], F32)
        cst = sb("cst", [128, 5 * 128], F32)
        onesD = sb("onesD", [128, 128], BF16)
        ones1 = sb("ones1", [128, 128], BF16)
        onesH = sb("onesH", [128, 128], BF16)
        ones32 = sb("ones32", [128, 128], F32)
        zero32 = sb("zero32", [128, 8], F32)
        wgate = sb("wgate", [128, KC, 32], BF16)
        bias_g = sb("bias_g", [128, max(n_ml, 1) * 32], F32)
        Gt = sb("Gt", [128, 8, 32], F32)
        Gt2 = sb("Gt2", [128, 8, 16], F32)
        ps = [st.enter_context(nc.psum_tensor("ps%d" % i, [128, 512], F32)) for i in range(8)]
        triLE = cst[:, 0:128]; triGE = cst[:, 128:256]; negFW = cst[:, 256:384]; negBW = cst[:, 384:512]
        cstb = sb("cstb", [128, 256], BF16)
        identb = sb("identb", [128, 128], BF16)
        negrep = sb("negrep", [128, 2, 512], BF16)

        P = Prog(nc)
        op = P.op
        op("dve", "memset", dict(ap=onesD[:], constant=1.0 / D), writes=["onesD"])
        op("dve", "memset", dict(ap=ones1[:], constant=1.0), writes=["ones1"])
        op("dve", "memset", dict(ap=onesH[:], constant=1.0 / 128), writes=["onesH"])
        op("dve", "memset", dict(ap=ones32[:], constant=1.0), writes=["ones32"])
        op("dve", "memset", dict(ap=zero32[:], constant=0.0), writes=["zero32"])
        op("sp", "dma_start", dict(out=gall[:], in_=gall_d), writes=["gall"], dma=True)
        op("sp", "dma_start", dict(out=cm[:], in_=cm_d), writes=["cm"], dma=True)
        op("sp", "dma_start", dict(out=cst[:], in_=cst_d), writes=["cst"], dma=True)
        op("sp", "dma_start", dict(out=bias_g[:], in_=bmg), writes=["bias_g"], dma=True)
        op("dve", "tensor_copy", dict(out=cstb[:], in_=cst[:, 0:256]), reads=["cst"], writes=["cstb"])
        op("dve", "tensor_copy", dict(out=identb[:], in_=cst[:, 512:640]), reads=["cst"], writes=["identb"])
        for d_ in range(2):
            op("dve", "tensor_copy", dict(out=negrep[:, d_, :].rearrange("p (k t) -> p k t", k=4), in_=cst[:, 256 + d_ * 128:384 + d_ * 128].unsqueeze(1).to_broadcast([128, 4, 128])), reads=["cst"], writes=["negrep"])
        zsv = zs.rearrange("(c p) t -> p c t", p=128)
        op("sp", "dma_start", dict(out=zsv[:, :, 0:1], in_=zero32[:, :].rearrange("p (c o) -> p c o", o=1), allow_slow_non_contiguous=True), reads=["zero32"], writes=[("zs", -1)], dma=True)
        op("sp", "dma_start", dict(out=zsv[:, :, NTOK + 1:NTOK + 2], in_=zero32[:, :].rearrange("p (c o) -> p c o", o=1), allow_slow_non_contiguous=True), reads=["zero32"], writes=[("zs", ntiles)], dma=True)

        fm = lambda ap: ap.rearrange("(c p) t -> p c t", p=128)
        xTv = fm(xT); yTv = fm(yT); xsv = fm(xs); bsv = fm(bs); osv = fm(o_s); hfwv = fm(hfw_s); hTv = fm(hT_s)
        qsv = fm(q_s); kTv = fm(kT_s)

        stage_no = [0]
        evn = [0]
        XK = lambda: [("x", c, th) for c in range(KC) for th in range(2)]
        XNK = lambda: [("xn", c, th) for c in range(KC) for th in range(2)]
        HK = lambda: [("h", c, th) for c in range(FC) for th in range(2)]

        def next_slot():
            s = stage_no[0] % 3
            stage_no[0] += 1
            return s

        def load_w(src3, width):
            s = next_slot()
            K = src3.shape[1]
            v = wp[s][:, 0:K * width].rearrange("p (k f) -> p k f", k=K)
            op("pool", "dma_start", dict(out=v, in_=src3), writes=[("w", s)], dma=True)
            return s, v

        def load_gu(wgl, wul, q):
            s = next_slot()
            c0, n = GU_STAGES[q]
            w = n * 128
            gv = wp[s][:, 0:8 * w].rearrange("p (k f) -> p k f", k=8)
            uv = wp[s][:, 8 * w:16 * w].rearrange("p (k f) -> p k f", k=8)
            op("pool", "dma_start", dict(out=gv, in_=wgl.rearrange("(k p) f -> p k f", p=128)[:, :, c0 * 128:c0 * 128 + w]), writes=[("w", s)], dma=True)
            op("pool", "dma_start", dict(out=uv, in_=wul.rearrange("(k p) f -> p k f", p=128)[:, :, c0 * 128:c0 * 128 + w]), writes=[("w", s)], dma=True)
            return s, gv, uv

        def rmsnorm(gcol0):
            sq = h
            for c in range(KC):
                op("act", "activation", dict(out=sq[:, c, :], in_=x[:, c, :], func=AF.Square),
                   reads=[("x", c, 0), ("x", c, 1)], writes=[("h", c, 0), ("h", c, 1)])
            for th in range(2):
                sl = slice(th * 512, (th + 1) * 512)
                for c in range(KC):
                    op("pe", "matmul", dict(out=ps[6 + th][:], lhsT=onesD[:], rhs=sq[:, c, sl], start=(c == 0), stop=(c == KC - 1)),
                       reads=[("h", c, th), "onesD"], writes=[("ps", 6 + th)])
                op("act", "activation", dict(out=rstd[:, sl], in_=ps[6 + th][:], func=AF.Ln, bias=EPS),
                   reads=[("ps", 6 + th)], writes=[("rstd", th)])
                op("act", "activation", dict(out=rstd[:, sl], in_=rstd[:, sl], func=AF.Exp, scale=-0.5),
                   reads=[("rstd", th)], writes=[("rstd", th)])
            for c in range(KC):
                for th in range(2):
                    sl = slice(th * 512, (th + 1) * 512)
                    op("dve", "scalar_tensor_tensor", dict(out=xn[:, c, sl], in0=x[:, c, sl], scalar=gall[:, gcol0 + c:gcol0 + c + 1], in1=rstd[:, sl], op0=ALU.mult, op1=ALU.mult),
                       reads=[("x", c, th), ("rstd", th), "gall"], writes=[("xn", c, th)])

        def ffn(wgl, wul, wdl, gcol0):
            rmsnorm(gcol0)
            k = 0
            for q in range(4):
                s, gv, uv = load_gu(wgl, wul, q)
                c0, n = GU_STAGES[q]
                for j in range(n):
                    hc = c0 + j
                    for th in range(2):
                        sl = slice(th * 512, (th + 1) * 512)
                        pg = ps[k % 2]; pu = ps[2 + k % 2]; sgt = sg[k % 2]
                        kg = ("ps", k % 2); ku = ("ps", 2 + k % 2); ks = ("sg", k % 2)
                        k += 1
                        for kc in range(KC):
                            op("pe", "matmul", dict(out=pg[:], lhsT=gv[:, kc, j * 128:(j + 1) * 128], rhs=xn[:, kc, sl], start=(kc == 0), stop=(kc == KC - 1)),
                               reads=[("w", s), ("xn", kc, th)], writes=[kg])
                        for kc in range(KC):
                            op("pe", "matmul", dict(out=pu[:], lhsT=uv[:, kc, j * 128:(j + 1) * 128], rhs=xn[:, kc, sl], start=(kc == 0), stop=(kc == KC - 1)),
                               reads=[("w", s), ("xn", kc, th)], writes=[ku])
                        op("act", "activation", dict(out=sgt[:], in_=pg[:], func=AF.Silu), reads=[kg], writes=[ks])
                        op("dve", "tensor_tensor", dict(out=h[:, hc, sl], in0=pu[:], in1=sgt[:], op=ALU.mult),
                           reads=[ku, ks], writes=[("h", hc, th)])
            k = 0
            for dh in range(2):
                s, dv = load_w(wdl.rearrange("(k p) d -> p k d", p=128)[:, :, dh * 512:(dh + 1) * 512], 512)
                for dcl in range(4):
                    dc = dh * 4 + dcl
                    for th in range(2):
                        sl = slice(th * 512, (th + 1) * 512)
                        pd = ps[4 + k % 2]; kd = ("ps", 4 + k % 2)
                        k += 1
                        for fc in range(FC):
                            op("pe", "matmul", dict(out=pd[:], lhsT=dv[:, fc, dcl * 128:(dcl + 1) * 128], rhs=h[:, fc, sl], start=(fc == 0), stop=(fc == FC - 1)),
                               reads=[("w", s), ("h", fc, th)], writes=[kd])
                        op("dve", "scalar_tensor_tensor", dict(out=x[:, dc, sl], in0=pd[:], scalar=0.5, in1=x[:, dc, sl], op0=ALU.mult, op1=ALU.add),
                           reads=[kd, ("x", dc, th)], writes=[("x", dc, th)])

        def load_x(src_v, t):
            tsl = slice(t * NT, (t + 1) * NT)
            for c in range(KC):
                op("sp", "dma_start", dict(out=x[:, c, :], in_=src_v[:, c, tsl]), reads=[("xs", t)], writes=[("x", c, 0), ("x", c, 1)], dma=True)

        def store_x(dst_v, t, key):
            tsl = slice(t * NT, (t + 1) * NT)
            for c in range(KC):
                op("sp", "dma_start", dict(out=dst_v[:, c, tsl], in_=x[:, c, :]), reads=[("x", c, 0), ("x", c, 1)], writes=[(key, t)], dma=True)

        pk = [0]

        def proj_fm(s, wv, col0, th, kin=KC):
            b = pk[0] % 4
            pk[0] += 1
            sl = slice(th * 512, (th + 1) * 512)
            for kc in range(kin):
                op("pe", "matmul", dict(out=ps[b][:], lhsT=wv[:, kc, col0:col0 + 128], rhs=xn[:, kc, sl], start=(kc == 0), stop=(kc == kin - 1)),
                   reads=[("w", s), ("xn", kc, th)], writes=[("ps", b)])
            return ps[b], ("ps", b)

        def evac_to_dram(pt, pkey, dst, dt, scale=None, key=None, eng=None):
            i = evn[0] % 4
            evn[0] += 1
            buf = ev32[i] if dt == F32 else ev16[i]
            bkey = ("ev32" if dt == F32 else "ev16", i)
            e = eng or ("act" if i % 2 == 0 else "dve")
            if e == "act":
                op("act", "activation", dict(out=buf[:], in_=pt[:], func=AF.Copy, scale=(1.0 if scale is None else scale)), reads=[pkey], writes=[bkey])
            else:
                op("dve", "tensor_scalar", dict(out=buf[:], in0=pt[:], scalar1=(1.0 if scale is None else scale), scalar2=None, op0=ALU.mult), reads=[pkey], writes=[bkey])
            op("sp", "dma_start", dict(out=dst, in_=buf[:]), reads=[bkey], writes=[key], dma=True)

        def conv_in(j, t):
            tsl0 = t * NT
            wsrc = wci[j].rearrange("(k p) f -> p k f", p=128)
            s, wv = load_w(wsrc[:, :, 0:D], D)
            for fc in range(KC):
                for th in range(2):
                    pt, pkey = proj_fm(s, wv, fc * 128, th)
                    evac_to_dram(pt, pkey, bsv[:, fc, tsl0 + th * 512: tsl0 + (th + 1) * 512], F32, key=("bs", t))
            cst32 = hstage32[:, 0:KC * NT].rearrange("p (c t) -> p c t", c=KC)
            s, wv = load_w(wsrc[:, :, D:2 * D], D)
            for fc in range(KC):
                for th in range(2):
                    pt, pkey = proj_fm(s, wv, fc * 128, th)
                    sl = slice(th * 512, (th + 1) * 512)
                    hk = [("h", 2 * fc, th), ("h", 2 * fc + 1, th)]
                    op("act", "activation", dict(out=cst32[:, fc, sl], in_=pt[:], func=AF.Copy), reads=[pkey], writes=hk)
            s, wv = load_w(wsrc[:, :, 2 * D:3 * D], D)
            for fc in range(KC):
                for th in range(2):
                    pt, pkey = proj_fm(s, wv, fc * 128, th)
                    sl = slice(th * 512, (th + 1) * 512)
                    hk = [("h", 2 * fc, th), ("h", 2 * fc + 1, th)]
                    i = evn[0] % 4
                    evn[0] += 1
                    op("dve", "tensor_tensor", dict(out=ev32[i][:], in0=pt[:], in1=cst32[:, fc, sl], op=ALU.mult), reads=[pkey] + hk, writes=[("ev32", i)])
                    op("sp", "dma_start", dict(out=zsv[:, fc, 1 + tsl0 + th * 512: 1 + tsl0 + (th + 1) * 512], in_=ev32[i][:]), reads=[("ev32", i)], writes=[("zs", t)], dma=True)

        def conv_mix(j, t):
            tsl0 = t * NT
            dwc = GO["dw"] + j * 24
            zst = hstage32[:, 0:KC * 514].rearrange("p (c t) -> p c t", c=KC)
            bst = hstage32[:, KC * 514:KC * 514 + KC * 512].rearrange("p (c t) -> p c t", c=KC)
            zkeys = [("h", c, th) for c in range(0, 9) for th in range(2)]
            bkeys = [("h", c, th) for c in range(8, 17) for th in range(2)]
            for th in range(2):
                a = tsl0 + th * 512
                rd = [("zs", t)]
                if th == 0:
                    rd.append(("zs", t - 1))
                else:
                    rd.append(("zs", t + 1))
                op("sp", "dma_start", dict(out=zst, in_=zsv[:, :, a:a + 514]), reads=rd, writes=zkeys, dma=True)
                op("sp", "dma_start", dict(out=bst, in_=bsv[:, :, a:a + 512]), reads=[("bs", t)], writes=bkeys, dma=True)
                if a % SEG == 0 and a > 0:
                    op("dve", "tensor_scalar", dict(out=zst[:, :, 0:1], in0=zst[:, :, 0:1], scalar1=cm[:, 0:1], scalar2=None, op0=ALU.mult), reads=zkeys + ["cm"], writes=zkeys)
                if (a + 512) % SEG == 0 and (a + 512) < NTOK:
                    op("dve", "tensor_scalar", dict(out=zst[:, :, 513:514], in0=zst[:, :, 513:514], scalar1=cm[:, 0:1], scalar2=None, op0=ALU.mult), reads=zkeys + ["cm"], writes=zkeys)
                sl = slice(th * 512, (th + 1) * 512)
                for c in range(KC):
                    e = "dve"
                    acc = rstd[:, (c % 2) * 512:(c % 2) * 512 + 512]
                    ak = ("rstd", c % 2)
                    w0 = gall[:, dwc + c:dwc + c + 1]; w1 = gall[:, dwc + 8 + c:dwc + 8 + c + 1]; w2 = gall[:, dwc + 16 + c:dwc + 16 + c + 1]
                    op(e, "tensor_scalar", dict(out=acc, in0=zst[:, c, 0:512], scalar1=w0, scalar2=None, op0=ALU.mult), reads=zkeys + ["gall"], writes=[ak])
                    op(e, "scalar_tensor_tensor", dict(out=acc, in0=zst[:, c, 1:513], scalar=w1, in1=acc, op0=ALU.mult, op1=ALU.add), reads=zkeys + [ak, "gall"], writes=[ak])
                    op(e, "scalar_tensor_tensor", dict(out=acc, in0=zst[:, c, 2:514], scalar=w2, in1=acc, op0=ALU.mult, op1=ALU.add), reads=zkeys + [ak, "gall"], writes=[ak])
                    op("pool", "tensor_tensor", dict(out=xn[:, c, sl], in0=acc, in1=bst[:, c, :], op=ALU.mult), reads=bkeys + [ak], writes=[("xn", c, th)])
            out_proj(wco[j], 1.0)

        def out_proj(wl, scale):
            s, wv = load_w(wl.rearrange("(k p) f -> p k f", p=128), D)
            for dc in range(KC):
                for th in range(2):
                    pt, pkey = proj_fm(s, wv, dc * 128, th)
                    sl = slice(th * 512, (th + 1) * 512)
                    op("dve", "scalar_tensor_tensor", dict(out=x[:, dc, sl], in0=pt[:], scalar=scale, in1=x[:, dc, sl], op0=ALU.mult, op1=ALU.add),
                       reads=[pkey, ("x", dc, th)], writes=[("x", dc, th)])

        def mlstm_in(j, t):
            tsl0 = t * NT
            wsrc = wmi[j].rearrange("(k p) f -> p k f", p=128)
            op("pool", "dma_start", dict(out=wgate[:], in_=wmg[j].rearrange("(k p) f -> p k f", p=128)), writes=["wgate"], dma=True)
            s, wv = load_w(wsrc[:, :, 0:D], D)
            for fc in range(4):
                for th in range(2):
                    pt, pkey = proj_fm(s, wv, fc * 128, th)
                    evac_to_dram(pt, pkey, qsv[:, fc, tsl0 + th * 512: tsl0 + (th + 1) * 512], F32, key=("q_s", t))
            for fc in range(4):
                for th in range(2):
                    pt, pkey = proj_fm(s, wv, 512 + fc * 128, th)
                    evac_to_dram(pt, pkey, kTv[:, fc, tsl0 + th * 512: tsl0 + (th + 1) * 512], BF16, scale=0.125, key=("kT_s", t))
            for tb in range(8):
                b = pk[0] % 4
                pk[0] += 1
                for kc in range(KC):
                    op("pe", "matmul", dict(out=ps[b][:], lhsT=xn[:, kc, tb * 128:(tb + 1) * 128], rhs=wv[:, kc, 512:1024], start=(kc == 0), stop=(kc == KC - 1)),
                       reads=[("w", s), ("xn", kc, tb // 4)], writes=[("ps", b)])
                evac_to_dram(ps[b], ("ps", b), ktm_s[tsl0 + tb * 128: tsl0 + (tb + 1) * 128, :], F32, scale=0.125, key=("ktm_s", t))
            gps = ps[6][:, 0:256].rearrange("p (b c) -> p b c", b=8)
            for tb in range(8):
                for kc in range(KC):
                    op("pe", "matmul", dict(out=ps[6][:, tb * 32:(tb + 1) * 32], lhsT=xn[:, kc, tb * 128:(tb + 1) * 128], rhs=wgate[:, kc, :], start=(kc == 0), stop=(kc == KC - 1)),
                       reads=["wgate", ("xn", kc, tb // 4)], writes=[("ps", 6)])
            bia = bias_g[:, j * 32:(j + 1) * 32]
            bia_b = bia.unsqueeze(1).to_broadcast([128, 8, 32])
            op("dve", "tensor_tensor", dict(out=Gt[:], in0=gps, in1=bia_b, op=ALU.add), reads=[("ps", 6), "bias_g"], writes=["Gt"])
            op("act", "activation", dict(out=Gt2[:], in_=Gt[:, :, 16:32], func=AF.Exp, scale=-1.0), reads=["Gt"], writes=["Gt2"])
            op("act", "activation", dict(out=Gt2[:], in_=Gt2[:], func=AF.Ln, bias=1.0), reads=["Gt2"], writes=["Gt2"])
            op("dve", "tensor_scalar", dict(out=Gt[:, :, 16:32], in0=Gt2[:], scalar1=-1.0, scalar2=None, op0=ALU.mult), reads=["Gt2", "Gt"], writes=["Gt"])
            op("sp", "dma_start", dict(out=g_s[tsl0:tsl0 + NT, :].rearrange("(b p) c -> p b c", p=128), in_=Gt[:]), reads=["Gt"], writes=[("g_s", t)], dma=True)
            s, wv = load_w(wsrc[:, :, D:2 * D], D)
            for tb in range(8):
                for chh in range(2):
                    b = pk[0] % 4
                    pk[0] += 1
                    for kc in range(KC):
                        op("pe", "matmul", dict(out=ps[b][:], lhsT=xn[:, kc, tb * 128:(tb + 1) * 128], rhs=wv[:, kc, chh * 512:(chh + 1) * 512], start=(kc == 0), stop=(kc == KC - 1)),
                           reads=[("w", s), ("xn", kc, tb // 4)], writes=[("ps", b)])
                    evac_to_dram(ps[b], ("ps", b), vtm_s[tsl0 + tb * 128: tsl0 + (tb + 1) * 128, chh * 512:(chh + 1) * 512], BF16, key=("vtm_s", t))
            s, wv = load_w(wsrc[:, :, 2 * D:3 * D], D)
            for fc in range(KC):
                for th in range(2):
                    pt, pkey = proj_fm(s, wv, fc * 128, th)
                    evac_to_dram(pt, pkey, osv[:, fc, tsl0 + th * 512: tsl0 + (th + 1) * 512], F32, key=("o_s", t))

        cur = [0]

        def carve(nelem, dt):
            nb = nelem * (4 if dt == F32 else 2)
            a0 = cur[0]
            cur[0] += nb // 2
            v = arena[:, a0:a0 + nb // 2]
            return v.bitcast(F32) if dt == F32 else v

        NB = 2
        c_q32 = [carve(4 * CH, F32) for _ in range(NB)]
        c_kT = [carve(4 * CH, BF16) for _ in range(NB)]
        c_ktm = [carve(512, F32) for _ in range(NB)]
        c_R = [carve(8 * 256, BF16) for _ in range(NB)]
        c_G = [carve(32, F32) for _ in range(NB)]
        c_o = [carve(8 * CH, F32) for _ in range(NB)]
        c_hfw = [carve(8 * CH, F32) for _ in range(NB)]
        c_qb = carve(8 * CH, BF16)
        c_Lmh = carve(8 * CH, BF16)
        c_Lml = carve(8 * CH, BF16)
        c_lfs = carve(16, BF16)
        c_lft = carve(8, F32)
        c_eb = carve(8 * CH, F32)
        c_Bm = carve(8 * CH, F32)
        c_E = carve(8 * CH, F32)
        c_sc = carve(8 * CH, BF16)
        c_qs = carve(8 * CH, BF16)
        c_kw = carve(512, BF16)
        c_S = carve(4 * 256, F32)
        c_Cb = [carve(4 * 256, BF16) for _ in range(2)]
        c_r = carve(8 * CH, F32)
        c_hs = carve(8 * CH, F32)
        c_sq = carve(8 * CH, BF16)
        c_rs = carve(8 * CH, F32)
        c_og = carve(8 * CH, F32)
        c_ho = [carve(8 * CH, BF16) for _ in range(2)]
        c_sm = carve(64, F32)
        assert cur[0] <= FC * NT + 3 * SLOT, cur[0]
        v3 = lambda ap, k: ap.rearrange("p (k t) -> p k t", k=k)

        def mlstm_core(j):
            ghc = GO["head"] + j * 8
            P.barrier()
            op("dve", "memset", dict(ap=c_qb, constant=0.0), writes=["cqb"])
            op("dve", "memset", dict(ap=c_qs, constant=0.0), writes=[("cqs", 0), ("cqs", 1)])
            for i in range(NB):
                op("dve", "memset", dict(ap=v3(c_R[i], 8)[:, :, 128:256], constant=1.0), writes=[("cR", i)])
            cbn = [0]
            for dirn in (0, 1):
                tri = triLE if dirn == 0 else triGE
                neg = negFW if dirn == 0 else negBW
                op("dve", "memset", dict(ap=c_S, constant=0.0), writes=["cS"])
                op("dve", "memset", dict(ap=c_Cb[cbn[0] % 2], constant=0.0), writes=[("cCb", cbn[0] % 2)])
                order = list(range(nchunks)) if dirn == 0 else list(range(nchunks - 1, -1, -1))
                for n_i, ck in enumerate(order):
                    bi = n_i % NB
                    a = ck * CH
                    t = a // NT
                    csl = slice(a, a + CH)
                    q32 = v3(c_q32[bi], 4); kTb = v3(c_kT[bi], 4); ktm = c_ktm[bi]; R = v3(c_R[bi], 8); G = c_G[bi]
                    ob = v3(c_o[bi], 8); hfwb = v3(c_hfw[bi], 8)
                    if CORE_STOP <= -1:
                        continue
                    op("sp", "dma_start", dict(out=q32, in_=qsv[:, :, csl]), reads=[("q_s", t)], writes=[("cq", bi)], dma=True)
                    op("sp", "dma_start", dict(out=kTb, in_=kTv[:, :, csl]), reads=[("kT_s", t)], writes=[("ckT", bi)], dma=True)
                    op("sp", "dma_start", dict(out=ktm, in_=ktm_s[csl, :]), reads=[("ktm_s", t)], writes=[("cktm", bi)], dma=True)
                    op("sp", "dma_start", dict(out=R[:, :, 0:128], in_=vtm_s[csl, :].rearrange("t (h v) -> t h v", h=8)), reads=[("vtm_s", t)], writes=[("cR", bi)], dma=True)
                    op("sp", "dma_start", dict(out=G, in_=g_s[csl, :]), reads=[("g_s", t)], writes=[("cG", bi)], dma=True)
                    if dirn == 1:
                        op("sp", "dma_start", dict(out=ob, in_=osv[:, :, csl]), reads=[("o_s", t)], writes=[("co", bi)], dma=True)
                        op("sp", "dma_start", dict(out=hfwb, in_=hfwv[:, :, csl]), reads=[("hfw_s", ck)], writes=[("chfw", bi)], dma=True)
                    if CORE_STOP <= 0:
                        continue
                    ig = G[:, dirn * 8:(dirn + 1) * 8]
                    lf = G[:, 16 + dirn * 8:16 + (dirn + 1) * 8]
                    seg_b = (a % SEG == 0 and a > 0) if dirn == 0 else ((a + CH) % SEG == 0 and (a + CH) < NTOK)
                    cb_cur = cbn[0] % 2
                    if seg_b:
                        op("dve", "tensor_scalar", dict(out=c_S, in0=c_S, scalar1=cm[:, 0:1], scalar2=None, op0=ALU.mult), reads=["cS", "cm"], writes=["cS"])
                        op("dve", "tensor_scalar", dict(out=c_Cb[cb_cur], in0=c_Cb[cb_cur], scalar1=cm[:, 0:1], scalar2=None, op0=ALU.mult), reads=[("cCb", cb_cur), "cm"], writes=[("cCb", cb_cur)])
                    trib = cstb[:, 0:128] if dirn == 0 else cstb[:, 128:256]
                    lfh = c_lfs[:, 0:8]; lfl = c_lfs[:, 8:16]
                    op("dve", "tensor_copy", dict(out=lfh, in_=lf), reads=[("cG", bi)], writes=["clfh"])
                    op("dve", "tensor_tensor", dict(out=c_lft, in0=lf, in1=lfh, op=ALU.subtract), reads=[("cG", bi), "clfh"], writes=["clft"])
                    op("dve", "tensor_copy", dict(out=lfl, in_=c_lft), reads=["clft"], writes=["clfl"])
                    op("pe", "matmul", dict(out=ps[7][:, 0:8], lhsT=trib, rhs=lfh, start=True, stop=False), reads=["cstb", "clfh", "clfl"], writes=[("ps", 7)])
                    op("pe", "matmul", dict(out=ps[7][:, 0:8], lhsT=trib, rhs=lfl, start=False, stop=True), reads=["cstb", "clfh", "clfl"], writes=[("ps", 7)])
                    op("pe", "matmul", dict(out=ps[7][:, 8:16], lhsT=ones1[:], rhs=lfh, start=True, stop=False), reads=["ones1", "clfh", "clfl"], writes=[("ps", 7)])
                    op("pe", "matmul", dict(out=ps[7][:, 8:16], lhsT=ones1[:], rhs=lfl, start=False, stop=True), reads=["ones1", "clfh", "clfl"], writes=[("ps", 7)])
                    na = c_sm[:, 0:8]; gsum = c_sm[:, 8:16]; wcol = c_sm[:, 16:24]; eg = c_sm[:, 24:32]
                    op("dve", "tensor_tensor", dict(out=na, in0=ig, in1=ps[7][:, 0:8], op=ALU.subtract), reads=[("ps", 7), ("cG", bi)], writes=["csm_na"])
                    op("dve", "tensor_tensor", dict(out=wcol, in0=ps[7][:, 8:16], in1=na, op=ALU.add), reads=[("ps", 7), "csm_na"], writes=["csm_w"])
                    op("act", "activation", dict(out=wcol, in_=wcol, func=AF.Exp), reads=["csm_w"], writes=["csm_w"])
                    op("act", "activation", dict(out=eg, in_=ps[7][:, 8:16], func=AF.Exp), reads=[("ps", 7)], writes=["csm_eg"])
                    if None == 'a':
                        continue
                    tri_b = trib.unsqueeze(1).to_broadcast([128, 8, 128])
                    op("dve", "tensor_tensor", dict(out=v3(c_Lmh, 8), in0=tri_b, in1=lfh.unsqueeze(2).to_broadcast([128, 8, 128]), op=ALU.mult), reads=["cstb", "clfh"], writes=["cLmh"])
                    op("dve", "tensor_tensor", dict(out=v3(c_Lml, 8), in0=tri_b, in1=lfl.unsqueeze(2).to_broadcast([128, 8, 128]), op=ALU.mult), reads=["cstb", "clfl"], writes=["cLml"])
                    for g4 in range(2):
                        op("pe", "matmul", dict(out=ps[g4][:], lhsT=ones1[:], rhs=c_Lmh[:, g4 * 512:(g4 + 1) * 512], start=True, stop=False), reads=["ones1", "cLmh", "cLml"], writes=[("ps", g4)])
                        op("pe", "matmul", dict(out=ps[g4][:], lhsT=ones1[:], rhs=c_Lml[:, g4 * 512:(g4 + 1) * 512], start=False, stop=True), reads=["ones1", "cLmh", "cLml"], writes=[("ps", g4)])
                        op("act", "activation", dict(out=c_eb[:, g4 * 512:(g4 + 1) * 512], in_=ps[g4][:], func=AF.Exp), reads=[("ps", g4)], writes=[("ceb", g4)])
                        mb = 2 + g4
                        op("pe", "matmul", dict(out=ps[mb][:], lhsT=ones1[:], rhs=c_Lmh[:, g4 * 512:(g4 + 1) * 512], start=True, stop=False), reads=["ones1", "cLmh", "cLml", "identb", "negrep"], writes=[("ps", mb)])
                        op("pe", "matmul", dict(out=ps[mb][:], lhsT=ones1[:], rhs=c_Lml[:, g4 * 512:(g4 + 1) * 512], start=False, stop=False), reads=["ones1", "cLmh", "cLml", "identb", "negrep"], writes=[("ps", mb)])
                        op("pe", "matmul", dict(out=ps[mb][:], lhsT=identb[:], rhs=negrep[:, dirn, :], start=False, stop=True), reads=["ones1", "cLmh", "cLml", "identb", "negrep"], writes=[("ps", mb)])
                        for hh in range(4):
                            hd = g4 * 4 + hh
                            op("act", "activation", dict(out=v3(c_E, 8)[:, hd, :], in_=ps[mb][:, hh * 128:(hh + 1) * 128], func=AF.Exp, bias=na[:, hd:hd + 1]), reads=[("ps", mb), "csm_na"], writes=[("cE", g4)])
                    if CORE_STOP <= 1:
                        continue
                    eb3 = v3(c_eb, 8)
                    qb3 = v3(c_qb, 8)
                    qs3 = v3(c_qs, 8)
                    for par in range(2):
                        prt = slice(par * 64, par * 64 + 64)
                        ebs = eb3[prt, par::2, :]
                        op("act", "activation", dict(out=qb3[prt, par::2, :], in_=q32[prt, :, :], func=AF.Copy), reads=[("cq", bi)], writes=["cqb"])
                        op("dve", "tensor_tensor", dict(out=qs3[prt, par::2, :], in0=q32[prt, :, :], in1=ebs, op=ALU.mult), reads=[("cq", bi), ("ceb", 0), ("ceb", 1)], writes=[("cqs", par)])
                    kt3 = ktm.rearrange("p (h k) -> p h k", h=8)
                    wc_b = wcol.unsqueeze(2).to_broadcast([128, 8, 64])
                    op("dve", "tensor_tensor", dict(out=c_kw.rearrange("p (h k) -> p h k", h=8), in0=kt3, in1=wc_b, op=ALU.mult), reads=[("cktm", bi), "csm_w"], writes=["ckw"])
                    if None == 'd':
                        continue
                    for hd in range(8):
                        op("pe", "matmul", dict(out=ps[2 + hd // 4][:, (hd % 4) * 128:(hd % 4 + 1) * 128], lhsT=kTb[:, hd // 2, :], rhs=qb3[:, hd, :], start=True, stop=True),
                           reads=[("ckT", bi), "cqb"], writes=[("ps", 2 + hd // 4)])
                    for g4 in range(2):
                        op("dve", "tensor_tensor", dict(out=c_sc[:, g4 * 512:(g4 + 1) * 512], in0=ps[2 + g4][:], in1=c_E[:, g4 * 512:(g4 + 1) * 512], op=ALU.mult), reads=[("ps", 2 + g4), ("cE", g4)], writes=[("csc", g4)])
                    if CORE_STOP <= 2:
                        continue
                    sc3 = v3(c_sc, 8)
                    Cb3 = v3(c_Cb[cb_cur], 4)
                    for hd in range(8):
                        prt = slice((hd % 2) * 64, (hd % 2) * 64 + 64)
                        g4 = hd // 4
                        oN = ps[4 + g4][:, (hd % 4) * 128:(hd % 4 + 1) * 128]
                        oD = ps[6 + g4][:, (hd % 4) * 128:(hd % 4 + 1) * 128]
                        rds = [("csc", g4), ("cR", bi), ("cqs", hd % 2), ("cCb", cb_cur), "ones1"]
                        op("pe", "matmul", dict(out=oN, lhsT=R[:, hd, 0:128], rhs=sc3[:, hd, :], start=True, stop=False), reads=rds, writes=[("ps", 4 + g4)])
                        op("pe", "matmul", dict(out=oN, lhsT=Cb3[:, hd // 2, 0:128], rhs=qs3[:, hd, :], start=False, stop=True), reads=rds, writes=[("ps", 4 + g4)])
                        op("pe", "matmul", dict(out=oD, lhsT=ones1[:], rhs=sc3[:, hd, :], start=True, stop=False), reads=rds, writes=[("ps", 6 + g4)])
                        op("pe", "matmul", dict(out=oD, lhsT=Cb3[:, hd // 2, 128:256], rhs=qs3[:, hd, :], start=False, stop=True), reads=rds, writes=[("ps", 6 + g4)])
                    for g4 in range(2):
                        gsl = slice(g4 * 512, (g4 + 1) * 512)
                        op("dve", "tensor_scalar", dict(out=c_r[:, gsl], in0=ps[6 + g4][:], scalar1=-1.0, scalar2=None, op0=ALU.mult), reads=[("ps", 6 + g4)], writes=[("cr", g4)])
                        op("dve", "scalar_tensor_tensor", dict(out=c_r[:, gsl], in0=ps[6 + g4][:], scalar=1.0, in1=c_r[:, gsl], op0=ALU.max, op1=ALU.max), reads=[("ps", 6 + g4), ("cr", g4)], writes=[("cr", g4)])
                        op("dve", "reciprocal", dict(out=c_r[:, gsl], in_=c_r[:, gsl]), reads=[("cr", g4)], writes=[("cr", g4)])
                        if dirn == 0:
                            op("dve", "tensor_tensor", dict(out=c_hfw[bi][:, gsl], in0=ps[4 + g4][:], in1=c_r[:, gsl], op=ALU.mult), reads=[("ps", 4 + g4), ("cr", g4)], writes=[("chfw", bi)])
                        else:
                            op("dve", "tensor_tensor", dict(out=c_hs[:, gsl], in0=ps[4 + g4][:], in1=c_r[:, gsl], op=ALU.mult), reads=[("ps", 4 + g4), ("cr", g4)], writes=[("chs", g4)])
                            op("pool", "tensor_tensor", dict(out=c_hs[:, gsl], in0=c_hs[:, gsl], in1=c_hfw[bi][:, gsl], op=ALU.add), reads=[("chs", g4), ("chfw", bi)], writes=[("chs", g4)])
                    if dirn == 0:
                        op("sp", "dma_start", dict(out=hfwv[:, :, csl], in_=hfwb), reads=[("chfw", bi)], writes=[("hfw_s", ck)], dma=True)
                    if CORE_STOP <= 3:
                        continue
                    cb_nxt = (cbn[0] + 1) % 2
                    for pr in range(4):
                        bnk = pr % 4
                        op("pe", "matmul", dict(out=ps[bnk][:], lhsT=c_kw[:, pr * 128:(pr + 1) * 128], rhs=c_R[bi][:, pr * 512:(pr + 1) * 512], start=True, stop=True),
                           reads=["ckw", ("cR", bi)], writes=[("ps", bnk)])
                        S3 = v3(c_S, 4)
                        op("dve", "scalar_tensor_tensor", dict(out=S3[0:64, pr, :], in0=S3[0:64, pr, :], scalar=eg[0:64, 2 * pr:2 * pr + 1], in1=ps[bnk][0:64, 0:256], op0=ALU.mult, op1=ALU.add),
                           reads=["cS", "csm_eg", ("ps", bnk)], writes=["cS"])
                        op("dve", "scalar_tensor_tensor", dict(out=S3[64:128, pr, :], in0=S3[64:128, pr, :], scalar=eg[64:128, 2 * pr + 1:2 * pr + 2], in1=ps[bnk][64:128, 256:512], op0=ALU.mult, op1=ALU.add),
                           reads=["cS", "csm_eg", ("ps", bnk)], writes=["cS"])
                    op("act", "activation", dict(out=c_Cb[cb_nxt], in_=c_S, func=AF.Copy), reads=["cS"], writes=[("cCb", cb_nxt)])
                    cbn[0] += 1
                    if CORE_STOP <= 4:
                        continue
                    if dirn == 1:
                        oi = n_i % 2
                        op("act", "activation", dict(out=c_sq, in_=c_hs, func=AF.Square), reads=[("chs", 0), ("chs", 1)], writes=["csq"])
                        op("act", "activation", dict(out=c_og, in_=c_o[bi], func=AF.Exp, scale=-1.0), reads=[("co", bi)], writes=["cog"])
                        op("pool", "tensor_scalar", dict(out=c_og, in0=c_og, scalar1=1.0, scalar2=None, op0=ALU.add), reads=["cog"], writes=["cog"])
                        op("dve", "reciprocal", dict(out=c_og, in_=c_og), reads=["cog"], writes=["cog"])
                        for g4 in range(2):
                            gsl = slice(g4 * 512, (g4 + 1) * 512)
                            op("pe", "matmul", dict(out=ps[4 + g4][:], lhsT=onesH[:], rhs=c_sq[:, gsl], start=True, stop=True), reads=["onesH", "csq"], writes=[("ps", 4 + g4)])
                            op("act", "activation", dict(out=c_rs[:, gsl], in_=ps[4 + g4][:], func=AF.Ln, bias=EPS), reads=[("ps", 4 + g4)], writes=[("crs", g4)])
                            op("act", "activation", dict(out=c_rs[:, gsl], in_=c_rs[:, gsl], func=AF.Exp, scale=-0.5), reads=[("crs", g4)], writes=[("crs", g4)])
                            op("dve", "tensor_tensor", dict(out=c_hs[:, gsl], in0=c_hs[:, gsl], in1=c_rs[:, gsl], op=ALU.mult), reads=[("chs", g4), ("crs", g4)], writes=[("chs", g4)])
                        ho3 = v3(c_ho[oi], 8); hs3 = v3(c_hs, 8); og3 = v3(c_og, 8)
                        for hd in range(8):
                            e = "dve"
                            op(e, "scalar_tensor_tensor", dict(out=ho3[:, hd, :], in0=hs3[:, hd, :], scalar=gall[:, ghc + hd:ghc + hd + 1], in1=og3[:, hd, :], op0=ALU.mult, op1=ALU.mult),
                               reads=[("chs", hd // 4), "cog", "gall"], writes=[("cho", oi)])
                        op("sp", "dma_start", dict(out=hTv[:, :, csl], in_=ho3), reads=[("cho", oi)], writes=[("hT_s", t)], dma=True)
            P.barrier()

        def ffn1(l):
            ffn(w1g[l], w1u[l], w1d[l], GO["ffn1"] + l * 8)

        def ffn2(l):
            ffn(w2g[l], w2u[l], w2d[l], GO["ffn2"] + l * 8)

        segs = []
        l = 0
        cur_seg = [("loadx", "in")]
        while l < DEPTH:
            j = l // 2
            cur_seg.append(("ffn1", l))
            if l % 2 == 0:
                cur_seg += [("norm_mix", l), ("conv_in", j), ("spill",)]
                segs.append(cur_seg)
                cur_seg = [("loadx", "xs"), ("conv_mix", j), ("ffn2", l)]
            else:
                cur_seg += [("norm_mix", l), ("mlstm_in", j), ("spill",)]
                segs.append(cur_seg)
                segs.append([("core", j)])
                cur_seg = [("loadx", "xs"), ("mlstm_out", j), ("ffn2", l)]
            l += 1
        cur_seg += [("final",)]
        segs.append(cur_seg)

        for si, seg in enumerate(segs):
            if debug_stop is not None and si >= debug_stop:
                break
            if seg[0][0] == "core":
                mlstm_core(seg[0][1])
                continue
            for t in range(ntiles):
                tsl = slice(t * NT, (t + 1) * NT)
                for step in seg:
                    k = step[0]
                    if k == "loadx":
                        load_x(xTv if step[1] == "in" else xsv, t)
                    elif k == "ffn1":
                        ffn1(step[1])
                    elif k == "ffn2":
                        ffn2(step[1])
                    elif k == "norm_mix":
                        rmsnorm(GO["mix"] + step[1] * 8)
                    elif k == "conv_in":
                        conv_in(step[1], t)
                    elif k == "conv_mix":
                        conv_mix(step[1], t)
                    elif k == "mlstm_in":
                        mlstm_in(step[1], t)
                    elif k == "mlstm_out":
                        for c in range(KC):
                            op("sp", "dma_start", dict(out=xn[:, c, :], in_=hTv[:, c, tsl]), reads=[("hT_s", t)], writes=[("xn", c, 0), ("xn", c, 1)], dma=True)
                        out_proj(wmo[step[1]], 1.0)
                    elif k == "spill":
                        store_x(xsv, t, "xs")
                    elif k == "final":
                        rmsnorm_final = True
                        sq = h
                        for c in range(KC):
                            op("act", "activation", dict(out=sq[:, c, :], in_=x[:, c, :], func=AF.Square), reads=[("x", c, 0), ("x", c, 1)], writes=[("h", c, 0), ("h", c, 1)])
                        for th in range(2):
                            sl = slice(th * 512, (th + 1) * 512)
                            for c in range(KC):
                                op("pe", "matmul", dict(out=ps[6 + th][:], lhsT=onesD[:], rhs=sq[:, c, sl], start=(c == 0), stop=(c == KC - 1)), reads=[("h", c, th), "onesD"], writes=[("ps", 6 + th)])
                            op("act", "activation", dict(out=rstd[:, sl], in_=ps[6 + th][:], func=AF.Ln, bias=EPS), reads=[("ps", 6 + th)], writes=[("rstd", th)])
                            op("act", "activation", dict(out=rstd[:, sl], in_=rstd[:, sl], func=AF.Exp, scale=-0.5), reads=[("rstd", th)], writes=[("rstd", th)])
                        gf = GO["final"]
                        for c in range(KC):
                            for th in range(2):
                                sl = slice(th * 512, (th + 1) * 512)
                                op("dve", "scalar_tensor_tensor", dict(out=x[:, c, sl], in0=x[:, c, sl], scalar=gall[:, gf + c:gf + c + 1], in1=rstd[:, sl], op0=ALU.mult, op1=ALU.mult),
                                   reads=[("x", c, th), ("rstd", th), "gall"], writes=[("x", c, th)])
                        store_x(yTv, t, "yT")
        if debug_stop is not None:
            pass
        P.emit(st)
    return nc


def host_inputs(inp, DEPTH, x_tok, cm_val):
    n_conv = (DEPTH + 1) // 2
    n_ml = DEPTH // 2
    GO = gall_layout(DEPTH)
    f = lambda a: np.ascontiguousarray(np.asarray(a, dtype=np.float32))
    gall = np.zeros((128, GO["_n"]), np.float32)
    colT = lambda v: v.reshape(-1, 8, 128).transpose(2, 0, 1).reshape(128, -1)
    gall[:, GO["ffn1"]:GO["ffn1"] + DEPTH * 8] = colT(f(inp["g_ffn1"])[:DEPTH])
    gall[:, GO["mix"]:GO["mix"] + DEPTH * 8] = colT(f(inp["g_mix"])[:DEPTH])
    gall[:, GO["ffn2"]:GO["ffn2"] + DEPTH * 8] = colT(f(inp["g_ffn2"])[:DEPTH])
    gall[:, GO["final"]:GO["final"] + 8] = colT(f(inp["g_final"])[None])
    if n_ml:
        gall[:, GO["head"]:GO["head"] + n_ml * 8] = colT(f(inp["g_mlstm_head"])[:n_ml])
    dw = f(inp["w_conv_dw"])[:n_conv]
    gall[:, GO["dw"]:GO["dw"] + n_conv * 24] = colT(dw.reshape(n_conv * 3, 1024))
    perm = np.concatenate([np.arange(0, 8), np.arange(16, 24), np.arange(8, 16), np.arange(24, 32)])
    m = {"xT": np.ascontiguousarray(x_tok.T), "gall": gall}
    for k_ in ("w_ffn1_gate", "w_ffn1_up", "w_ffn1_down", "w_ffn2_gate", "w_ffn2_up", "w_ffn2_down"):
        m[k_] = f(inp[k_])[:DEPTH]
    m["w_conv_in"] = f(inp["w_conv_in"])[:n_conv]
    m["w_conv_out"] = f(inp["w_conv_out"])[:n_conv]
    nm = max(n_ml, 1)
    if n_ml == 0:
        inp = dict(inp)
        inp["w_mlstm_in"] = np.zeros((1, 1024, 3072), np.float32); inp["w_mlstm_out"] = np.zeros((1, 1024, 1024), np.float32)
        inp["w_mlstm_gate"] = np.zeros((1, 1024, 32), np.float32); inp["b_mlstm_gate"] = np.zeros((1, 32), np.float32)
    wmi = f(inp["w_mlstm_in"])[:nm]
    m["w_mlstm_in"] = np.ascontiguousarray(np.concatenate([wmi[:, :, 0:1024], wmi[:, :, 1024:3072]], axis=2))
    m["w_mlstm_out"] = f(inp["w_mlstm_out"])[:nm]
    m["w_mlstm_gate"] = np.ascontiguousarray(f(inp["w_mlstm_gate"])[:nm][:, :, perm])
    bg = f(inp["b_mlstm_gate"])[:nm][:, perm].reshape(1, nm * 32)
    m["b_mlstm_gate"] = np.ascontiguousarray(np.broadcast_to(bg, (128, nm * 32)))
    m["cm"] = np.full((128, 1), cm_val, np.float32)
    jj = np.arange(128)[:, None]; tt = np.arange(128)[None, :]
    triLE = (jj <= tt).astype(np.float32); triGE = (jj >= tt).astype(np.float32)
    negFW = np.where(jj <= tt, 0.0, NEG).astype(np.float32); negBW = np.where(jj >= tt, 0.0, NEG).astype(np.float32)
    m["cst"] = np.ascontiguousarray(np.concatenate([triLE, triGE, negFW, negBW, np.eye(128, dtype=np.float32)], axis=1))
    return m


NTOK_CORE = 16384
SEGLEN = 4096
NLAYER = 4
_KEYS = ["g_ffn1", "w_ffn1_gate", "w_ffn1_up", "w_ffn1_down", "g_mix", "w_conv_in", "w_conv_dw", "w_conv_out",
         "w_mlstm_in", "w_mlstm_gate", "b_mlstm_gate", "g_mlstm_head", "w_mlstm_out", "g_ffn2", "w_ffn2_gate",
         "w_ffn2_up", "w_ffn2_down", "g_final"]


def kernel(x_prompt, x_sample, **w):
    x_prompt = np.asarray(x_prompt, dtype=np.float32)
    x_sample = np.asarray(x_sample, dtype=np.float32)
    inp = {k: np.asarray(w[k]) for k in _KEYS}
    nc = build(NTOK_CORE, SEGLEN, NLAYER)
    streams = [(x_sample[0], 1.0), (x_sample[1], 1.0),
               (x_prompt[0:4].reshape(NTOK_CORE, D), 0.0), (x_prompt[4:8].reshape(NTOK_CORE, D), 0.0)]
    streams += [streams[2]] * 4
    shared = host_inputs(inp, NLAYER, streams[0][0], 0.0)
    in_maps = []
    for xt, cmv in streams:
        m = dict(shared)
        m["xT"] = np.ascontiguousarray(xt.T)
        m["cm"] = np.full((128, 1), cmv, np.float32)
        in_maps.append(m)
    res = run_bass_kernel_spmd(nc, in_maps, core_ids=list(range(8)))
    ys = [np.ascontiguousarray(res.results[c]["yT"].T) for c in range(4)]
    y_sample = np.stack([ys[0], ys[1]], axis=0).astype(np.float32)
    y_prompt = np.concatenate([ys[2].reshape(4, SEGLEN, D), ys[3].reshape(4, SEGLEN, D)], axis=0).astype(np.float32)
    return (y_prompt, y_sample)
```

```python
from contextlib import ExitStack
from concourse.bass_utils import run_bass_kernel_spmd
import numpy as np
import concourse.bass as bass
import concourse.mybir as mybir

F32 = mybir.dt.float32
BF16 = mybir.dt.bfloat16
AF = mybir.ActivationFunctionType
ALU = mybir.AluOpType

COMPUTE = ("pe", "act", "dve", "pool")


class Op:
    __slots__ = ("eng", "fn", "reads", "writes", "dma", "idx", "sig", "waits", "lane", "lane_prev", "cc")

    def __init__(self, eng, fn, reads, writes, dma):
        self.eng = eng
        self.fn = fn
        self.reads = reads
        self.writes = writes
        self.dma = dma
        self.sig = None
        self.waits = None
        self.lane = None


class Prog:
    def __init__(self, nc, n_lanes=6, same_engine_sync=True):
        self.nc = nc
        self.ops = []
        self.n_lanes = n_lanes
        self.same_engine_sync = same_engine_sync

    def barrier(self):
        self.ops.append(None)

    def op(self, eng, name, kw, reads=(), writes=(), dma=False, cc=False):
        o = Op(eng, (name, kw), tuple(reads), tuple(writes), dma)
        o.cc = cc
        self.ops.append(o)
        return o

    def analyze(self):
        raw = self.ops
        ops = []
        bar_at = []
        for o in raw:
            if o is None:
                bar_at.append(len(ops))
            else:
                ops.append(o)
        self.ops = ops
        bar_set = set(bar_at)
        last_of = {}
        pending_bar = {}
        q_n0 = {}
        last_w = {}
        readers = {}
        deps_all = []
        needed = set()
        for i, o in enumerate(ops):
            o.idx = i
            if i in bar_set:
                alld = list(last_of.values())
                for e in ("pe", "act", "dve", "pool", "sp"):
                    pending_bar[e] = list(alld)
            deps = set()
            if pending_bar.get(o.eng):
                deps.update(pending_bar[o.eng])
                pending_bar[o.eng] = None
            if o.cc:
                last_of[("cc", i)] = i
            elif o.dma:
                k0 = q_n0.get(o.eng, 0)
                q_n0[o.eng] = k0 + 1
                last_of[("lane", o.eng, k0 % self.n_lanes)] = i
            else:
                last_of[o.eng] = i
            for r in o.reads:
                d = last_w.get(r)
                if d is not None:
                    deps.add(d)
                if type(r) is tuple and r[0] == "ps":
                    for rd in readers.get(r, ()):
                        if ops[rd].eng != o.eng:
                            deps.add(rd)
            for w in o.writes:
                d = last_w.get(w)
                if d is not None:
                    deps.add(d)
                rs = readers.get(w)
                if rs:
                    deps.update(rs)
            deps.discard(i)
            bar_deps = ()
            fd = []
            for d in deps:
                po = ops[d]
                if po.eng == o.eng and not po.dma and not o.dma:
                    if o.eng == "pe" or not self.same_engine_sync or d in bar_deps:
                        continue
                fd.append(d)
                needed.add(d)
            deps_all.append(fd)
            for r in o.reads:
                readers.setdefault(r, []).append(i)
            for w in o.writes:
                last_w[w] = i
                readers[w] = []
        cnt = {e: 0 for e in COMPUTE}
        lane_cnt = {}
        q_n = {}
        self.cc_ids = []
        for i, o in enumerate(ops):
            if o.cc:
                o.lane_prev = 0
                o.sig = (("cc", i), 1)
                self.cc_ids.append(i)
            elif o.dma:
                k = q_n.get(o.eng, 0)
                q_n[o.eng] = k + 1
                lane = (o.eng, k % self.n_lanes)
                o.lane = lane
                c = lane_cnt.get(lane, 0)
                o.lane_prev = 16 * c
                lane_cnt[lane] = c + 1
                o.sig = (("lane",) + lane, 16 * (c + 1))
            elif i in needed:
                cnt[o.eng] += 1
                o.sig = (("eng", o.eng), cnt[o.eng])
        waited = {}
        for i, o in enumerate(ops):
            w = {}
            for d in deps_all[i]:
                s, v = ops[d].sig
                if w.get(s, 0) < v:
                    w[s] = v
            if o.dma and o.lane_prev > 0:
                s = ("lane",) + o.lane
                if w.get(s, 0) < o.lane_prev:
                    w[s] = o.lane_prev
            fw = []
            for s, v in w.items():
                key = (o.eng, s)
                if waited.get(key, 0) >= v:
                    continue
                waited[key] = v
                fw.append((s, v))
            o.waits = fw
        self.final_counts = (cnt, lane_cnt)

    def emit(self, stack):
        nc = self.nc
        self.analyze()
        sems = {}
        for e in COMPUTE:
            sems[("eng", e)] = stack.enter_context(nc.semaphore("s_" + e))
        for q in ("sp", "act", "pool"):
            for l in range(self.n_lanes):
                sems[("lane", q, l)] = stack.enter_context(nc.semaphore("l_%s%d" % (q, l)))
        for i in self.cc_ids:
            sems[("cc", i)] = stack.enter_context(nc.semaphore("cc_%d" % i))
        per = {e: [] for e in ("pe", "act", "dve", "pool", "sp")}
        for o in self.ops:
            per[o.eng].append(o)
        block = stack.enter_context(nc.Block())
        cnt, lane_cnt = self.final_counts

        def run(eng_name, eng):
            for o in per[eng_name]:
                for s, v in o.waits:
                    eng.wait_ge(sems[s], v)
                ins = getattr(eng, o.fn[0])(**o.fn[1])
                if o.sig is not None:
                    s, v = o.sig
                    if o.cc:
                        ins.then_inc(sems[s])
                    else:
                        ins.then_inc(sems[s], 16 if o.dma else 1)
            for (q, l), c in lane_cnt.items():
                if q == eng_name:
                    eng.wait_ge(sems[("lane", q, l)], 16 * c)
            if eng_name == "pool":
                for i in self.cc_ids:
                    eng.wait_ge(sems[("cc", i)], 1)

        @block.tensor
        def _(e):
            run("pe", e)

        @block.scalar
        def _(e):
            run("act", e)

        @block.vector
        def _(e):
            run("dve", e)

        @block.gpsimd
        def _(e):
            run("pool", e)

        @block.sync
        def _(e):
            run("sp", e)


D = 1024; DFF = 2816; NT = 1024; KC = 8; FC = 22; NH = 8
EPS = 1e-6
GU_STAGES = [(0, 6), (6, 6), (12, 5), (17, 5)]
SLOT = 2 * 8 * 768
CH = 128
NEG = -30000.0
CORE_STOP = 9

def gall_layout(depth):
    off = {}
    o = 0
    for nm, n in (("ffn1", depth * 8), ("mix", depth * 8), ("ffn2", depth * 8), ("final", 8),
                  ("head", (depth // 2) * 8), ("dw", ((depth + 1) // 2) * 3 * 8)):
        off[nm] = o
        o += n
    off["_n"] = o
    return off


def build(NTOK, SEG, DEPTH, debug_stop=None, XCH=False):
    ntiles = NTOK // NT
    nchunks = NTOK // CH
    n_conv = (DEPTH + 1) // 2
    n_ml = DEPTH // 2
    GO = gall_layout(DEPTH)
    nc = bass.Bass("TRN2", target_bir_lowering=False)
    dt_in = lambda name, shape, dt=F32: nc.dram_tensor(name, shape, dt, kind="ExternalInput").ap()
    xT = dt_in("xT", [D, NTOK])
    w1g = dt_in("w_ffn1_gate", [DEPTH, D, DFF]); w1u = dt_in("w_ffn1_up", [DEPTH, D, DFF]); w1d = dt_in("w_ffn1_down", [DEPTH, DFF, D])
    w2g = dt_in("w_ffn2_gate", [DEPTH, D, DFF]); w2u = dt_in("w_ffn2_up", [DEPTH, D, DFF]); w2d = dt_in("w_ffn2_down", [DEPTH, DFF, D])
    wci = dt_in("w_conv_in", [n_conv, D, 3 * D]); wco = dt_in("w_conv_out", [n_conv, D, D])
    wmi = dt_in("w_mlstm_in", [max(n_ml, 1), D, 3 * D]); wmo = dt_in("w_mlstm_out", [max(n_ml, 1), D, D])
    wmg = dt_in("w_mlstm_gate", [max(n_ml, 1), D, 32])
    bmg = dt_in("b_mlstm_gate", [128, max(n_ml, 1) * 32])
    gall_d = dt_in("gall", [128, GO["_n"]])
    cm_d = dt_in("cm", [128, 1])
    cst_d = dt_in("cst", [128, 5 * 128])
    yT = nc.dram_tensor("yT", [D, NTOK], F32, kind="ExternalOutput").ap()
    xch_bufs = {}
    if XCH:
        msk_d = dt_in("msk", [128, 48])
        for j_ in range(n_ml):
            xch_bufs[("ml", j_)] = (nc.dram_tensor("xin_ml%d" % j_, [128, 2064], F32).ap(), nc.dram_tensor("xout_ml%d" % j_, [1024, 2064], F32).ap())
        for j_ in range(n_conv):
            xch_bufs[("cv", j_)] = (nc.dram_tensor("xin_cv%d" % j_, [128, 16], F32).ap(), nc.dram_tensor("xout_cv%d" % j_, [1024, 16], F32).ap())
    xs = nc.dram_tensor("xs", [D, NTOK], F32).ap()
    zs = nc.dram_tensor("zs", [D, NTOK + 2], F32).ap()
    bs = nc.dram_tensor("bs", [D, NTOK], F32).ap()
    q_s = nc.dram_tensor("q_s", [512, NTOK], F32).ap()
    kT_s = nc.dram_tensor("kT_s", [512, NTOK], BF16).ap()
    ktm_s = nc.dram_tensor("ktm_s", [NTOK, 512], F32).ap()
    vtm_s = nc.dram_tensor("vtm_s", [NTOK, D], BF16).ap()
    o_s = nc.dram_tensor("o_s", [D, NTOK], F32).ap()
    g_s = nc.dram_tensor("g_s", [NTOK, 32], F32).ap()
    hfw_s = nc.dram_tensor("hfw_s", [D, NTOK], F32).ap()
    hT_s = nc.dram_tensor("hT_s", [D, NTOK], BF16).ap()

    st = ExitStack()
    with st:
        sb = lambda name, shape, dt: st.enter_context(nc.sbuf_tensor("sb_" + name, shape, dt))
        x = sb("x", [128, KC, NT], F32)
        xn = sb("xn", [128, KC, NT], BF16)
        arena = sb("arena", [128, FC * NT + 3 * SLOT], BF16)
        h = arena[:, 0:FC * NT].rearrange("p (k t) -> p k t", k=FC)
        wp = [arena[:, FC * NT + i * SLOT: FC * NT + (i + 1) * SLOT] for i in range(3)]
        hstage32 = arena[:, 0:FC * NT].bitcast(F32)
        rstd = sb("rstd", [128, NT], F32)
        sg = [sb("sg%d" % i, [128, 512], F32) for i in range(2)]
        ev32 = [sb("ev32_%d" % i, [128, 512], F32) for i in range(4)]
        ev16 = [sb("ev16_%d" % i, [128, 512], BF16) for i in range(4)]
        gall = sb("gall", [128, GO["_n"]], F32)
        cm = sb("cm", [128, 1], F32)
        cst = sb("cst", [128, 5 * 128], F32)
        onesD = sb("onesD", [128, 128], BF16)
        ones1 = sb("ones1", [128, 128], BF16)
        onesH = sb("onesH", [128, 128], BF16)
        ones32 = sb("ones32", [128, 128], F32)
        zero32 = sb("zero32", [128, 8], F32)
        wgate = sb("wgate", [128, KC, 32], BF16)
        bias_g = sb("bias_g", [128, max(n_ml, 1) * 32], F32)
        Gt = sb("Gt", [128, 8, 32], F32)
        Gt2 = sb("Gt2", [128, 8, 16], F32)
        ps = [st.enter_context(nc.psum_tensor("ps%d" % i, [128, 512], F32)) for i in range(8)]
        triLE = cst[:, 0:128]; triGE = cst[:, 128:256]; negFW = cst[:, 256:384]; negBW = cst[:, 384:512]
        cstb = sb("cstb", [128, 256], BF16)
        identb = sb("identb", [128, 128], BF16)
        negrep = sb("negrep", [128, 2, 512], BF16)

        P = Prog(nc, same_engine_sync=True)
        op = P.op
        if XCH:
            msk = sb("msk", [128, 48], F32)
            cx = sb("cx", [128, 16], F32)
            hx = sb("hx", [128, 8, 16], F32)
            hl = sb("hl", [128, 8], F32)
            hr = sb("hr", [128, 8], F32)
            op("sp", "dma_start", dict(out=msk[:], in_=msk_d), writes=["msk"], dma=True)
        op("dve", "memset", dict(ap=onesD[:], constant=1.0 / D), writes=["onesD"])
        op("dve", "memset", dict(ap=ones1[:], constant=1.0), writes=["ones1"])
        op("dve", "memset", dict(ap=onesH[:], constant=1.0 / 128), writes=["onesH"])
        op("dve", "memset", dict(ap=ones32[:], constant=1.0), writes=["ones32"])
        op("dve", "memset", dict(ap=zero32[:], constant=0.0), writes=["zero32"])
        op("sp", "dma_start", dict(out=gall[:], in_=gall_d), writes=["gall"], dma=True)
        op("sp", "dma_start", dict(out=cm[:], in_=cm_d), writes=["cm"], dma=True)
        op("sp", "dma_start", dict(out=cst[:], in_=cst_d), writes=["cst"], dma=True)
        op("sp", "dma_start", dict(out=bias_g[:], in_=bmg), writes=["bias_g"], dma=True)
        op("dve", "tensor_copy", dict(out=cstb[:], in_=cst[:, 0:256]), reads=["cst"], writes=["cstb"])
        op("dve", "tensor_copy", dict(out=identb[:], in_=cst[:, 512:640]), reads=["cst"], writes=["identb"])
        for d_ in range(2):
            op("dve", "tensor_copy", dict(out=negrep[:, d_, :].rearrange("p (k t) -> p k t", k=4), in_=cst[:, 256 + d_ * 128:384 + d_ * 128].unsqueeze(1).to_broadcast([128, 4, 128])), reads=["cst"], writes=["negrep"])
        zsv = zs.rearrange("(c p) t -> p c t", p=128)
        op("sp", "dma_start", dict(out=zsv[:, :, 0:1], in_=zero32[:, :].rearrange("p (c o) -> p c o", o=1), allow_slow_non_contiguous=True), reads=["zero32"], writes=[("zs", -1)], dma=True)
        op("sp", "dma_start", dict(out=zsv[:, :, NTOK + 1:NTOK + 2], in_=zero32[:, :].rearrange("p (c o) -> p c o", o=1), allow_slow_non_contiguous=True), reads=["zero32"], writes=[("zs", ntiles)], dma=True)

        fm = lambda ap: ap.rearrange("(c p) t -> p c t", p=128)
        xTv = fm(xT); yTv = fm(yT); xsv = fm(xs); bsv = fm(bs); osv = fm(o_s); hfwv = fm(hfw_s); hTv = fm(hT_s)
        qsv = fm(q_s); kTv = fm(kT_s)

        stage_no = [0]
        evn = [0]
        XK = lambda: [("x", c, th) for c in range(KC) for th in range(2)]
        XNK = lambda: [("xn", c, th) for c in range(KC) for th in range(2)]
        HK = lambda: [("h", c, th) for c in range(FC) for th in range(2)]

        def next_slot():
            s = stage_no[0] % 3
            stage_no[0] += 1
            return s

        def load_w(src3, width):
            s = next_slot()
            K = src3.shape[1]
            v = wp[s][:, 0:K * width].rearrange("p (k f) -> p k f", k=K)
            op("pool", "dma_start", dict(out=v, in_=src3), writes=[("w", s)], dma=True)
            return s, v

        def load_gu(wgl, wul, q):
            s = next_slot()
            c0, n = GU_STAGES[q]
            w = n * 128
            gv = wp[s][:, 0:8 * w].rearrange("p (k f) -> p k f", k=8)
            uv = wp[s][:, 8 * w:16 * w].rearrange("p (k f) -> p k f", k=8)
            op("pool", "dma_start", dict(out=gv, in_=wgl.rearrange("(k p) f -> p k f", p=128)[:, :, c0 * 128:c0 * 128 + w]), writes=[("w", s)], dma=True)
            op("pool", "dma_start", dict(out=uv, in_=wul.rearrange("(k p) f -> p k f", p=128)[:, :, c0 * 128:c0 * 128 + w]), writes=[("w", s)], dma=True)
            return s, gv, uv

        def rmsnorm(gcol0):
            sq = h
            for c in range(KC):
                op("act", "activation", dict(out=sq[:, c, :], in_=x[:, c, :], func=AF.Square),
                   reads=[("x", c, 0), ("x", c, 1)], writes=[("h", c, 0), ("h", c, 1)])
            for th in range(2):
                sl = slice(th * 512, (th + 1) * 512)
                for c in range(KC):
                    op("pe", "matmul", dict(out=ps[6 + th][:], lhsT=onesD[:], rhs=sq[:, c, sl], start=(c == 0), stop=(c == KC - 1)),
                       reads=[("h", c, th), "onesD"], writes=[("ps", 6 + th)])
                op("act", "activation", dict(out=rstd[:, sl], in_=ps[6 + th][:], func=AF.Ln, bias=EPS),
                   reads=[("ps", 6 + th)], writes=[("rstd", th)])
                op("act", "activation", dict(out=rstd[:, sl], in_=rstd[:, sl], func=AF.Exp, scale=-0.5),
                   reads=[("rstd", th)], writes=[("rstd", th)])
            for c in range(KC):
                for th in range(2):
                    sl = slice(th * 512, (th + 1) * 512)
                    op("dve", "scalar_tensor_tensor", dict(out=xn[:, c, sl], in0=x[:, c, sl], scalar=gall[:, gcol0 + c:gcol0 + c + 1], in1=rstd[:, sl], op0=ALU.mult, op1=ALU.mult),
                       reads=[("x", c, th), ("rstd", th), "gall"], writes=[("xn", c, th)])

        def ffn(wgl, wul, wdl, gcol0):
            rmsnorm(gcol0)
            k = 0
            for q in range(4):
                s, gv, uv = load_gu(wgl, wul, q)
                c0, n = GU_STAGES[q]
                for j in range(n):
                    hc = c0 + j
                    for th in range(2):
                        sl = slice(th * 512, (th + 1) * 512)
                        pg = ps[k % 2]; pu = ps[2 + k % 2]; sgt = sg[k % 2]
                        kg = ("ps", k % 2); ku = ("ps", 2 + k % 2); ks = ("sg", k % 2)
                        k += 1
                        for kc in range(KC):
                            op("pe", "matmul", dict(out=pg[:], lhsT=gv[:, kc, j * 128:(j + 1) * 128], rhs=xn[:, kc, sl], start=(kc == 0), stop=(kc == KC - 1)),
                               reads=[("w", s), ("xn", kc, th)], writes=[kg])
                        for kc in range(KC):
                            op("pe", "matmul", dict(out=pu[:], lhsT=uv[:, kc, j * 128:(j + 1) * 128], rhs=xn[:, kc, sl], start=(kc == 0), stop=(kc == KC - 1)),
                               reads=[("w", s), ("xn", kc, th)], writes=[ku])
                        op("act", "activation", dict(out=sgt[:], in_=pg[:], func=AF.Silu), reads=[kg], writes=[ks])
                        op("dve", "tensor_tensor", dict(out=h[:, hc, sl], in0=pu[:], in1=sgt[:], op=ALU.mult),
                           reads=[ku, ks], writes=[("h", hc, th)])
            k = 0
            for dh in range(2):
                s, dv = load_w(wdl.rearrange("(k p) d -> p k d", p=128)[:, :, dh * 512:(dh + 1) * 512], 512)
                for dcl in range(4):
                    dc = dh * 4 + dcl
                    for th in range(2):
                        sl = slice(th * 512, (th + 1) * 512)
                        pd = ps[4 + k % 2]; kd = ("ps", 4 + k % 2)
                        k += 1
                        for fc in range(FC):
                            op("pe", "matmul", dict(out=pd[:], lhsT=dv[:, fc, dcl * 128:(dcl + 1) * 128], rhs=h[:, fc, sl], start=(fc == 0), stop=(fc == FC - 1)),
                               reads=[("w", s), ("h", fc, th)], writes=[kd])
                        op("dve", "scalar_tensor_tensor", dict(out=x[:, dc, sl], in0=pd[:], scalar=0.5, in1=x[:, dc, sl], op0=ALU.mult, op1=ALU.add),
                           reads=[kd, ("x", dc, th)], writes=[("x", dc, th)])

        def load_x(src_v, t):
            tsl = slice(t * NT, (t + 1) * NT)
            for c in range(KC):
                op("sp", "dma_start", dict(out=x[:, c, :], in_=src_v[:, c, tsl]), reads=[("xs", t)], writes=[("x", c, 0), ("x", c, 1)], dma=True)

        def store_x(dst_v, t, key):
            tsl = slice(t * NT, (t + 1) * NT)
            for c in range(KC):
                op("sp", "dma_start", dict(out=dst_v[:, c, tsl], in_=x[:, c, :]), reads=[("x", c, 0), ("x", c, 1)], writes=[(key, t)], dma=True)

        pk = [0]

        def proj_fm(s, wv, col0, th, kin=KC):
            b = pk[0] % 4
            pk[0] += 1
            sl = slice(th * 512, (th + 1) * 512)
            for kc in range(kin):
                op("pe", "matmul", dict(out=ps[b][:], lhsT=wv[:, kc, col0:col0 + 128], rhs=xn[:, kc, sl], start=(kc == 0), stop=(kc == kin - 1)),
                   reads=[("w", s), ("xn", kc, th)], writes=[("ps", b)])
            return ps[b], ("ps", b)

        base32 = [(ev32[i][:], [("ev32", i)]) for i in range(4)]
        base16 = [(ev16[i][:], [("ev16", i)]) for i in range(4)]
        EVP = {F32: list(base32), BF16: list(base16)}
        evc = {F32: 0, BF16: 0}

        def set_ev_pools(extra32, extra16):
            EVP[F32] = list(base32) + [(hstage32[:, k * 512:(k + 1) * 512], [("h", k, 0), ("h", k, 1)]) for k in extra32]
            EVP[BF16] = list(base16) + [(arena[:, k * 1024 + th * 512:k * 1024 + (th + 1) * 512], [("h", k, th)]) for k in extra16 for th in range(2)]

        def evac_to_dram(pt, pkey, dst, dt, scale=None, key=None, eng=None):
            pool = EVP[dt]
            i = evc[dt] % len(pool)
            evc[dt] += 1
            buf, bkeys = pool[i]
            e = eng or ("act" if evc[dt] % 2 == 0 else "dve")
            if e == "act":
                op("act", "activation", dict(out=buf, in_=pt[:], func=AF.Copy, scale=(1.0 if scale is None else scale)), reads=[pkey], writes=bkeys)
            else:
                op("dve", "tensor_scalar", dict(out=buf, in0=pt[:], scalar1=(1.0 if scale is None else scale), scalar2=None, op0=ALU.mult), reads=[pkey], writes=bkeys)
            op("sp", "dma_start", dict(out=dst, in_=buf), reads=bkeys, writes=[key], dma=True)

        def conv_in(j, t):
            tsl0 = t * NT
            set_ev_pools(list(range(16, 22)), [])
            wsrc = wci[j].rearrange("(k p) f -> p k f", p=128)
            s, wv = load_w(wsrc[:, :, 0:D], D)
            for fc in range(KC):
                for th in range(2):
                    pt, pkey = proj_fm(s, wv, fc * 128, th)
                    evac_to_dram(pt, pkey, bsv[:, fc, tsl0 + th * 512: tsl0 + (th + 1) * 512], F32, key=("bs", t))
            cst32 = hstage32[:, 0:KC * NT].rearrange("p (c t) -> p c t", c=KC)
            s, wv = load_w(wsrc[:, :, D:2 * D], D)
            for fc in range(KC):
                for th in range(2):
                    pt, pkey = proj_fm(s, wv, fc * 128, th)
                    sl = slice(th * 512, (th + 1) * 512)
                    hk = [("h", 2 * fc, th), ("h", 2 * fc + 1, th)]
                    op("act", "activation", dict(out=cst32[:, fc, sl], in_=pt[:], func=AF.Copy), reads=[pkey], writes=hk)
            s, wv = load_w(wsrc[:, :, 2 * D:3 * D], D)
            for fc in range(KC):
                for th in range(2):
                    pt, pkey = proj_fm(s, wv, fc * 128, th)
                    sl = slice(th * 512, (th + 1) * 512)
                    hk = [("h", 2 * fc, th), ("h", 2 * fc + 1, th)]
                    pool = EVP[F32]
                    i = evc[F32] % len(pool)
                    evc[F32] += 1
                    zbuf, zk = pool[i]
                    op("dve", "tensor_tensor", dict(out=zbuf, in0=pt[:], in1=cst32[:, fc, sl], op=ALU.mult), reads=[pkey] + hk, writes=zk)
                    op("sp", "dma_start", dict(out=zsv[:, fc, 1 + tsl0 + th * 512: 1 + tsl0 + (th + 1) * 512], in_=zbuf), reads=zk, writes=[("zs", t)], dma=True)

        def conv_mix(j, t):
            tsl0 = t * NT
            dwc = GO["dw"] + j * 24
            zst = hstage32[:, 0:KC * 514].rearrange("p (c t) -> p c t", c=KC)
            bst = hstage32[:, KC * 514:KC * 514 + KC * 512].rearrange("p (c t) -> p c t", c=KC)
            zkeys = [("h", c, th) for c in range(0, 9) for th in range(2)]
            bkeys = [("h", c, th) for c in range(8, 17) for th in range(2)]
            for th in range(2):
                a = tsl0 + th * 512
                rd = [("zs", t)]
                if th == 0:
                    rd.append(("zs", t - 1))
                else:
                    rd.append(("zs", t + 1))
                op("sp", "dma_start", dict(out=zst, in_=zsv[:, :, a:a + 514]), reads=rd, writes=zkeys, dma=True)
                op("sp", "dma_start", dict(out=bst, in_=bsv[:, :, a:a + 512]), reads=[("bs", t)], writes=bkeys, dma=True)
                if XCH:
                    if a == SEG:
                        op("dve", "tensor_copy", dict(out=zst[:, :, 0:1], in_=hl[:, :].unsqueeze(2)), reads=zkeys + ["hl"], writes=zkeys)
                    if a + 512 == SEG:
                        op("dve", "tensor_scalar", dict(out=zst[:, :, 513:514], in0=zst[:, :, 513:514], scalar1=0.0, scalar2=None, op0=ALU.mult), reads=zkeys, writes=zkeys)
                    if a + 512 == NTOK:
                        op("dve", "tensor_copy", dict(out=zst[:, :, 513:514], in_=hr[:, :].unsqueeze(2)), reads=zkeys + ["hr"], writes=zkeys)
                else:
                    if a % SEG == 0 and a > 0:
                        op("dve", "tensor_scalar", dict(out=zst[:, :, 0:1], in0=zst[:, :, 0:1], scalar1=cm[:, 0:1], scalar2=None, op0=ALU.mult), reads=zkeys + ["cm"], writes=zkeys)
                    if (a + 512) % SEG == 0 and (a + 512) < NTOK:
                        op("dve", "tensor_scalar", dict(out=zst[:, :, 513:514], in0=zst[:, :, 513:514], scalar1=cm[:, 0:1], scalar2=None, op0=ALU.mult), reads=zkeys + ["cm"], writes=zkeys)
                sl = slice(th * 512, (th + 1) * 512)
                for c in range(KC):
                    e = "dve"
                    acc = rstd[:, (c % 2) * 512:(c % 2) * 512 + 512]
                    ak = ("rstd", c % 2)
                    w0 = gall[:, dwc + c:dwc + c + 1]; w1 = gall[:, dwc + 8 + c:dwc + 8 + c + 1]; w2 = gall[:, dwc + 16 + c:dwc + 16 + c + 1]
                    op(e, "tensor_scalar", dict(out=acc, in0=zst[:, c, 0:512], scalar1=w0, scalar2=None, op0=ALU.mult), reads=zkeys + ["gall"], writes=[ak])
                    op(e, "scalar_tensor_tensor", dict(out=acc, in0=zst[:, c, 1:513], scalar=w1, in1=acc, op0=ALU.mult, op1=ALU.add), reads=zkeys + [ak, "gall"], writes=[ak])
                    op(e, "scalar_tensor_tensor", dict(out=acc, in0=zst[:, c, 2:514], scalar=w2, in1=acc, op0=ALU.mult, op1=ALU.add), reads=zkeys + [ak, "gall"], writes=[ak])
                    op("pool", "tensor_tensor", dict(out=xn[:, c, sl], in0=acc, in1=bst[:, c, :], op=ALU.mult), reads=bkeys + [ak], writes=[("xn", c, th)])
            out_proj(wco[j], 1.0)

        def out_proj(wl, scale):
            s, wv = load_w(wl.rearrange("(k p) f -> p k f", p=128), D)
            for th in range(2):
                for dc in range(KC):
                    pt, pkey = proj_fm(s, wv, dc * 128, th)
                    sl = slice(th * 512, (th + 1) * 512)
                    op("dve", "scalar_tensor_tensor", dict(out=x[:, dc, sl], in0=pt[:], scalar=scale, in1=x[:, dc, sl], op0=ALU.mult, op1=ALU.add),
                       reads=[pkey, ("x", dc, th)], writes=[("x", dc, th)])

        def mlstm_in(j, t):
            tsl0 = t * NT
            set_ev_pools(list(range(0, 12)), list(range(12, 22)))
            wsrc = wmi[j].rearrange("(k p) f -> p k f", p=128)
            op("pool", "dma_start", dict(out=wgate[:], in_=wmg[j].rearrange("(k p) f -> p k f", p=128)), writes=["wgate"], dma=True)
            s, wv = load_w(wsrc[:, :, 0:D], D)
            for fc in range(4):
                for th in range(2):
                    pt, pkey = proj_fm(s, wv, fc * 128, th)
                    evac_to_dram(pt, pkey, qsv[:, fc, tsl0 + th * 512: tsl0 + (th + 1) * 512], F32, key=("q_s", t))
            for fc in range(4):
                for th in range(2):
                    pt, pkey = proj_fm(s, wv, 512 + fc * 128, th)
                    evac_to_dram(pt, pkey, kTv[:, fc, tsl0 + th * 512: tsl0 + (th + 1) * 512], BF16, scale=0.125, key=("kT_s", t))
            for tb in range(8):
                b = pk[0] % 4
                pk[0] += 1
                for kc in range(KC):
                    op("pe", "matmul", dict(out=ps[b][:], lhsT=xn[:, kc, tb * 128:(tb + 1) * 128], rhs=wv[:, kc, 512:1024], start=(kc == 0), stop=(kc == KC - 1)),
                       reads=[("w", s), ("xn", kc, tb // 4)], writes=[("ps", b)])
                evac_to_dram(ps[b], ("ps", b), ktm_s[tsl0 + tb * 128: tsl0 + (tb + 1) * 128, :], F32, scale=0.125, key=("ktm_s", t))
            gps = ps[6][:, 0:256].rearrange("p (b c) -> p b c", b=8)
            for tb in range(8):
                for kc in range(KC):
                    op("pe", "matmul", dict(out=ps[6][:, tb * 32:(tb + 1) * 32], lhsT=xn[:, kc, tb * 128:(tb + 1) * 128], rhs=wgate[:, kc, :], start=(kc == 0), stop=(kc == KC - 1)),
                       reads=["wgate", ("xn", kc, tb // 4)], writes=[("ps", 6)])
            bia = bias_g[:, j * 32:(j + 1) * 32]
            bia_b = bia.unsqueeze(1).to_broadcast([128, 8, 32])
            op("dve", "tensor_tensor", dict(out=Gt[:], in0=gps, in1=bia_b, op=ALU.add), reads=[("ps", 6), "bias_g"], writes=["Gt"])
            op("act", "activation", dict(out=Gt2[:], in_=Gt[:, :, 16:32], func=AF.Exp, scale=-1.0), reads=["Gt"], writes=["Gt2"])
            op("act", "activation", dict(out=Gt2[:], in_=Gt2[:], func=AF.Ln, bias=1.0), reads=["Gt2"], writes=["Gt2"])
            op("dve", "tensor_scalar", dict(out=Gt[:, :, 16:32], in0=Gt2[:], scalar1=-1.0, scalar2=None, op0=ALU.mult), reads=["Gt2", "Gt"], writes=["Gt"])
            op("sp", "dma_start", dict(out=g_s[tsl0:tsl0 + NT, :].rearrange("(b p) c -> p b c", p=128), in_=Gt[:]), reads=["Gt"], writes=[("g_s", t)], dma=True)
            s, wv = load_w(wsrc[:, :, D:2 * D], D)
            for tb in range(8):
                for chh in range(2):
                    b = pk[0] % 4
                    pk[0] += 1
                    for kc in range(KC):
                        op("pe", "matmul", dict(out=ps[b][:], lhsT=xn[:, kc, tb * 128:(tb + 1) * 128], rhs=wv[:, kc, chh * 512:(chh + 1) * 512], start=(kc == 0), stop=(kc == KC - 1)),
                           reads=[("w", s), ("xn", kc, tb // 4)], writes=[("ps", b)])
                    evac_to_dram(ps[b], ("ps", b), vtm_s[tsl0 + tb * 128: tsl0 + (tb + 1) * 128, chh * 512:(chh + 1) * 512], BF16, key=("vtm_s", t))
            s, wv = load_w(wsrc[:, :, 2 * D:3 * D], D)
            for fc in range(KC):
                for th in range(2):
                    pt, pkey = proj_fm(s, wv, fc * 128, th)
                    evac_to_dram(pt, pkey, osv[:, fc, tsl0 + th * 512: tsl0 + (th + 1) * 512], F32, key=("o_s", t))

        cur = [0]

        def carve(nelem, dt):
            nb = nelem * (4 if dt == F32 else 2)
            a0 = cur[0]
            cur[0] += nb // 2
            v = arena[:, a0:a0 + nb // 2]
            return v.bitcast(F32) if dt == F32 else v

        NB = 2
        c_q32 = [carve(4 * CH, F32) for _ in range(NB)]
        c_kT = [carve(4 * CH, BF16) for _ in range(NB)]
        c_ktm = [carve(512, F32) for _ in range(NB)]
        c_R = [carve(8 * 256, BF16) for _ in range(NB)]
        c_G = [carve(32, F32) for _ in range(NB)]
        c_o = [carve(8 * CH, F32) for _ in range(NB)]
        c_hfw = [carve(8 * CH, F32) for _ in range(NB)]
        c_qb = carve(8 * CH, BF16)
        c_Lmh = carve(8 * CH, BF16)
        c_Lml = carve(8 * CH, BF16)
        c_lfs = carve(16, BF16)
        c_lft = carve(8, F32)
        c_eb = carve(8 * CH, F32)
        c_Bm = carve(8 * CH, F32)
        c_E = carve(8 * CH, F32)
        c_scP = [carve(8 * CH, BF16) for _ in range(2)]
        c_qsP = [carve(8 * CH, BF16) for _ in range(2)]
        c_kwP = [carve(512, BF16) for _ in range(2)]
        c_S = carve(4 * 256, F32)
        c_Cb = [carve(4 * 256, BF16) for _ in range(2)]
        c_r = carve(8 * CH, F32)
        c_hs = carve(8 * CH, F32)
        c_sq = carve(8 * CH, BF16)
        c_rs = carve(8 * CH, F32)
        c_og = carve(8 * CH, F32)
        c_ho = [carve(8 * CH, BF16) for _ in range(2)]
        c_smP = [carve(64, F32) for _ in range(2)]
        c_gt = carve(8, F32)
        if XCH:
            xflat = x[:, :, :].rearrange("p c t -> p (c t)")
            xnflat = xn[:, :, :].rearrange("p c t -> p (c t)").bitcast(F32)
            c_xst = xflat[:, 0:2064]
            c_xr = [xflat[:, 2064 * (i + 1):2064 * (i + 2)] for i in range(2)]
            c_Tf = xnflat[:, 0:1024]; c_Tb = xnflat[:, 1024:2048]
            c_U = xnflat[:, 2048:3072]
            c_A = carve(8, F32)
        assert cur[0] <= FC * NT + 3 * SLOT, cur[0]
        v3 = lambda ap, k: ap.rearrange("p (k t) -> p k t", k=k)

        def mlstm_core(j):
            ghc = GO["head"] + j * 8
            P.barrier()
            op("dve", "memset", dict(ap=c_qb, constant=0.0), writes=["cqb"])
            for pp_ in range(2):
                op("dve", "memset", dict(ap=c_qsP[pp_], constant=0.0), writes=[("cqs", 0, pp_), ("cqs", 1, pp_)])
            for i in range(NB):
                op("dve", "memset", dict(ap=v3(c_R[i], 8)[:, :, 128:256], constant=1.0), writes=[("cR", i)])
            cbn = [0]
            S3 = v3(c_S, 4)

            def state_update(bi, eg, cb_nxt, c_kw, pp):
                for pr in range(4):
                    bnk = pr % 4
                    op("pe", "matmul", dict(out=ps[bnk][:], lhsT=c_kw[:, pr * 128:(pr + 1) * 128], rhs=c_R[bi][:, pr * 512:(pr + 1) * 512], start=True, stop=True),
                       reads=[("ckw", pp), ("cR", bi)], writes=[("ps", bnk)])
                    op("dve", "scalar_tensor_tensor", dict(out=S3[0:64, pr, :], in0=S3[0:64, pr, :], scalar=eg[0:64, 2 * pr:2 * pr + 1], in1=ps[bnk][0:64, 0:256], op0=ALU.mult, op1=ALU.add),
                       reads=["cS", ("csm_eg", pp), ("ps", bnk)], writes=["cS"])
                    op("dve", "scalar_tensor_tensor", dict(out=S3[64:128, pr, :], in0=S3[64:128, pr, :], scalar=eg[64:128, 2 * pr + 1:2 * pr + 2], in1=ps[bnk][64:128, 256:512], op0=ALU.mult, op1=ALU.add),
                       reads=["cS", ("csm_eg", pp), ("ps", bnk)], writes=["cS"])
                if cb_nxt is not None:
                    op("act", "activation", dict(out=c_Cb[cb_nxt], in_=c_S, func=AF.Copy), reads=["cS"], writes=[("cCb", cb_nxt)])

            def chunk_body(dirn, n_i, ck, mode, gi=0):
                pp = gi % 2
                c_sc = c_scP[pp]; c_qs = c_qsP[pp]; c_kw = c_kwP[pp]; c_sm = c_smP[pp]
                tri = triLE if dirn == 0 else triGE
                neg = negFW if dirn == 0 else negBW
                if True:
                    bi = gi % NB if mode == 'full' else n_i % NB
                    a = ck * CH
                    t = a // NT
                    csl = slice(a, a + CH)
                    q32 = v3(c_q32[bi], 4); kTb = v3(c_kT[bi], 4); ktm = c_ktm[bi]; R = v3(c_R[bi], 8); G = c_G[bi]
                    ob = v3(c_o[bi], 8); hfwb = v3(c_hfw[bi], 8)
                    if CORE_STOP <= -1:
                        return
                    if mode == "full":
                        op("sp", "dma_start", dict(out=q32, in_=qsv[:, :, csl]), reads=[("q_s", t)], writes=[("cq", bi)], dma=True)
                        op("sp", "dma_start", dict(out=kTb, in_=kTv[:, :, csl]), reads=[("kT_s", t)], writes=[("ckT", bi)], dma=True)
                    op("sp", "dma_start", dict(out=ktm, in_=ktm_s[csl, :]), reads=[("ktm_s", t)], writes=[("cktm", bi)], dma=True)
                    op("sp", "dma_start", dict(out=R[:, :, 0:128], in_=vtm_s[csl, :].rearrange("t (h v) -> t h v", h=8)), reads=[("vtm_s", t)], writes=[("cR", bi)], dma=True)
                    op("sp", "dma_start", dict(out=G, in_=g_s[csl, :]), reads=[("g_s", t)], writes=[("cG", bi)], dma=True)
                    if dirn == 1 and mode == "full":
                        op("sp", "dma_start", dict(out=ob, in_=osv[:, :, csl]), reads=[("o_s", t)], writes=[("co", bi)], dma=True)
                        op("sp", "dma_start", dict(out=hfwb, in_=hfwv[:, :, csl]), reads=[("hfw_s", ck)], writes=[("chfw", bi)], dma=True)
                    if CORE_STOP <= 0:
                        return
                    ig = G[:, dirn * 8:(dirn + 1) * 8]
                    lf = G[:, 16 + dirn * 8:16 + (dirn + 1) * 8]
                    trib = cstb[:, 0:128] if dirn == 0 else cstb[:, 128:256]
                    lfh = c_lfs[:, 0:8]; lfl = c_lfs[:, 8:16]
                    op("dve", "tensor_copy", dict(out=lfh, in_=lf), reads=[("cG", bi)], writes=["clfh"])
                    op("dve", "tensor_tensor", dict(out=c_lft, in0=lf, in1=lfh, op=ALU.subtract), reads=[("cG", bi), "clfh"], writes=["clft"])
                    op("dve", "tensor_copy", dict(out=lfl, in_=c_lft), reads=["clft"], writes=["clfl"])
                    op("pe", "matmul", dict(out=ps[7][:, 0:8], lhsT=trib, rhs=lfh, start=True, stop=False), reads=["cstb", "clfh", "clfl"], writes=[("ps", 7)])
                    op("pe", "matmul", dict(out=ps[7][:, 0:8], lhsT=trib, rhs=lfl, start=False, stop=True), reads=["cstb", "clfh", "clfl"], writes=[("ps", 7)])
                    op("pe", "matmul", dict(out=ps[7][:, 8:16], lhsT=ones1[:], rhs=lfh, start=True, stop=False), reads=["ones1", "clfh", "clfl"], writes=[("ps", 7)])
                    op("pe", "matmul", dict(out=ps[7][:, 8:16], lhsT=ones1[:], rhs=lfl, start=False, stop=True), reads=["ones1", "clfh", "clfl"], writes=[("ps", 7)])
                    na = c_sm[:, 0:8]; gsum = c_sm[:, 8:16]; wcol = c_sm[:, 16:24]; eg = c_sm[:, 24:32]
                    op("dve", "tensor_tensor", dict(out=na, in0=ig, in1=ps[7][:, 0:8], op=ALU.subtract), reads=[("ps", 7), ("cG", bi)], writes=[("csm_na", pp)])
                    op("dve", "tensor_tensor", dict(out=wcol, in0=ps[7][:, 8:16], in1=na, op=ALU.add), reads=[("ps", 7), ("csm_na", pp)], writes=[("csm_w", pp)])
                    op("act", "activation", dict(out=wcol, in_=wcol, func=AF.Exp), reads=[("csm_w", pp)], writes=[("csm_w", pp)])
                    op("act", "activation", dict(out=eg, in_=ps[7][:, 8:16], func=AF.Exp), reads=[("ps", 7)], writes=[("csm_eg", pp)])
                    if mode == "state":
                        op("dve", "tensor_tensor", dict(out=c_gt, in0=c_gt, in1=ps[7][:, 8:16], op=ALU.add), reads=[("ps", 7), "cgt"], writes=["cgt"])
                        kt3 = ktm.rearrange("p (h k) -> p h k", h=8)
                        wc_b = wcol.unsqueeze(2).to_broadcast([128, 8, 64])
                        op("dve", "tensor_tensor", dict(out=c_kw.rearrange("p (h k) -> p h k", h=8), in0=kt3, in1=wc_b, op=ALU.mult), reads=[("cktm", bi), ("csm_w", pp)], writes=[("ckw", pp)])
                        state_update(bi, eg, None, c_kw, pp)
                        return
                    if None == 'a':
                        return
                    yield
                    tri_b = trib.unsqueeze(1).to_broadcast([128, 8, 128])
                    op("dve", "tensor_tensor", dict(out=v3(c_Lmh, 8), in0=tri_b, in1=lfh.unsqueeze(2).to_broadcast([128, 8, 128]), op=ALU.mult), reads=["cstb", "clfh"], writes=["cLmh"])
                    op("dve", "tensor_tensor", dict(out=v3(c_Lml, 8), in0=tri_b, in1=lfl.unsqueeze(2).to_broadcast([128, 8, 128]), op=ALU.mult), reads=["cstb", "clfl"], writes=["cLml"])
                    for g4 in range(2):
                        op("pe", "matmul", dict(out=ps[g4][:], lhsT=ones1[:], rhs=c_Lmh[:, g4 * 512:(g4 + 1) * 512], start=True, stop=False), reads=["ones1", "cLmh", "cLml"], writes=[("ps", g4)])
                        op("pe", "matmul", dict(out=ps[g4][:], lhsT=ones1[:], rhs=c_Lml[:, g4 * 512:(g4 + 1) * 512], start=False, stop=True), reads=["ones1", "cLmh", "cLml"], writes=[("ps", g4)])
                        op("act", "activation", dict(out=c_eb[:, g4 * 512:(g4 + 1) * 512], in_=ps[g4][:], func=AF.Exp), reads=[("ps", g4)], writes=[("ceb", g4)])
                        mb = 2 + g4
                        op("pe", "matmul", dict(out=ps[mb][:], lhsT=ones1[:], rhs=c_Lmh[:, g4 * 512:(g4 + 1) * 512], start=True, stop=False), reads=["ones1", "cLmh", "cLml", "identb", "negrep"], writes=[("ps", mb)])
                        op("pe", "matmul", dict(out=ps[mb][:], lhsT=ones1[:], rhs=c_Lml[:, g4 * 512:(g4 + 1) * 512], start=False, stop=False), reads=["ones1", "cLmh", "cLml", "identb", "negrep"], writes=[("ps", mb)])
                        op("pe", "matmul", dict(out=ps[mb][:], lhsT=identb[:], rhs=negrep[:, dirn, :], start=False, stop=True), reads=["ones1", "cLmh", "cLml", "identb", "negrep"], writes=[("ps", mb)])
                        for hh in range(4):
                            hd = g4 * 4 + hh
                            op("act", "activation", dict(out=v3(c_E, 8)[:, hd, :], in_=ps[mb][:, hh * 128:(hh + 1) * 128], func=AF.Exp, bias=na[:, hd:hd + 1]), reads=[("ps", mb), ("csm_na", pp)], writes=[("cE", g4)])
                    if CORE_STOP <= 1:
                        return
                    yield
                    eb3 = v3(c_eb, 8)
                    qb3 = v3(c_qb, 8)
                    qs3 = v3(c_qs, 8)
                    for par in range(2):
                        prt = slice(par * 64, par * 64 + 64)
                        ebs = eb3[prt, par::2, :]
                        op("act", "activation", dict(out=qb3[prt, par::2, :], in_=q32[prt, :, :], func=AF.Copy), reads=[("cq", bi)], writes=["cqb"])
                        op("dve", "tensor_tensor", dict(out=qs3[prt, par::2, :], in0=q32[prt, :, :], in1=ebs, op=ALU.mult), reads=[("cq", bi), ("ceb", 0), ("ceb", 1)], writes=[("cqs", par, pp)])
                    kt3 = ktm.rearrange("p (h k) -> p h k", h=8)
                    wc_b = wcol.unsqueeze(2).to_broadcast([128, 8, 64])
                    op("dve", "tensor_tensor", dict(out=c_kw.rearrange("p (h k) -> p h k", h=8), in0=kt3, in1=wc_b, op=ALU.mult), reads=[("cktm", bi), ("csm_w", pp)], writes=[("ckw", pp)])
                    if None == 'd':
                        return
                    for hd in range(8):
                        op("pe", "matmul", dict(out=ps[2 + hd // 4][:, (hd % 4) * 128:(hd % 4 + 1) * 128], lhsT=kTb[:, hd // 2, :], rhs=qb3[:, hd, :], start=True, stop=True),
                           reads=[("ckT", bi), "cqb"], writes=[("ps", 2 + hd // 4)])
                    for g4 in range(2):
                        op("dve", "tensor_tensor", dict(out=c_sc[:, g4 * 512:(g4 + 1) * 512], in0=ps[2 + g4][:], in1=c_E[:, g4 * 512:(g4 + 1) * 512], op=ALU.mult), reads=[("ps", 2 + g4), ("cE", g4)], writes=[("csc", g4, pp)])
                    if CORE_STOP <= 2:
                        return
                    yield
                    cb_cur = cbn[0] % 2
                    sc3 = v3(c_sc, 8)
                    Cb3 = v3(c_Cb[cb_cur], 4)
                    for hd in range(8):
                        prt = slice((hd % 2) * 64, (hd % 2) * 64 + 64)
                        g4 = hd // 4
                        oN = ps[4 + g4][:, (hd % 4) * 128:(hd % 4 + 1) * 128]
                        oD = ps[6 + g4][:, (hd % 4) * 128:(hd % 4 + 1) * 128]
                        rds = [("csc", g4, pp), ("cR", bi), ("cqs", hd % 2, pp), ("cCb", cb_cur), "ones1"]
                        op("pe", "matmul", dict(out=oN, lhsT=R[:, hd, 0:128], rhs=sc3[:, hd, :], start=True, stop=False), reads=rds, writes=[("ps", 4 + g4)])
                        op("pe", "matmul", dict(out=oN, lhsT=Cb3[:, hd // 2, 0:128], rhs=qs3[:, hd, :], start=False, stop=True), reads=rds, writes=[("ps", 4 + g4)])
                        op("pe", "matmul", dict(out=oD, lhsT=ones1[:], rhs=sc3[:, hd, :], start=True, stop=False), reads=rds, writes=[("ps", 6 + g4)])
                        op("pe", "matmul", dict(out=oD, lhsT=Cb3[:, hd // 2, 128:256], rhs=qs3[:, hd, :], start=False, stop=True), reads=rds, writes=[("ps", 6 + g4)])
                    for g4 in range(2):
                        gsl = slice(g4 * 512, (g4 + 1) * 512)
                        op("dve", "tensor_scalar", dict(out=c_r[:, gsl], in0=ps[6 + g4][:], scalar1=-1.0, scalar2=None, op0=ALU.mult), reads=[("ps", 6 + g4)], writes=[("cr", g4)])
                        op("dve", "scalar_tensor_tensor", dict(out=c_r[:, gsl], in0=ps[6 + g4][:], scalar=1.0, in1=c_r[:, gsl], op0=ALU.max, op1=ALU.max), reads=[("ps", 6 + g4), ("cr", g4)], writes=[("cr", g4)])
                        op("dve", "reciprocal", dict(out=c_r[:, gsl], in_=c_r[:, gsl]), reads=[("cr", g4)], writes=[("cr", g4)])
                        if dirn == 0:
                            op("dve", "tensor_tensor", dict(out=c_hfw[bi][:, gsl], in0=ps[4 + g4][:], in1=c_r[:, gsl], op=ALU.mult), reads=[("ps", 4 + g4), ("cr", g4)], writes=[("chfw", bi)])
                        else:
                            op("dve", "tensor_tensor", dict(out=c_hs[:, gsl], in0=ps[4 + g4][:], in1=c_r[:, gsl], op=ALU.mult), reads=[("ps", 4 + g4), ("cr", g4)], writes=[("chs", g4)])
                            op("pool", "tensor_tensor", dict(out=c_hs[:, gsl], in0=c_hs[:, gsl], in1=c_hfw[bi][:, gsl], op=ALU.add), reads=[("chs", g4), ("chfw", bi)], writes=[("chs", g4)])
                    if dirn == 0:
                        op("sp", "dma_start", dict(out=hfwv[:, :, csl], in_=hfwb), reads=[("chfw", bi)], writes=[("hfw_s", ck)], dma=True)
                    if CORE_STOP <= 3:
                        return
                    yield
                    cb_nxt = (cbn[0] + 1) % 2
                    state_update(bi, eg, cb_nxt, c_kw, pp)
                    cbn[0] += 1
                    if CORE_STOP <= 4:
                        return
                    yield
                    if dirn == 1:
                        oi = gi % 2
                        op("act", "activation", dict(out=c_sq, in_=c_hs, func=AF.Square), reads=[("chs", 0), ("chs", 1)], writes=["csq"])
                        op("act", "activation", dict(out=c_og, in_=c_o[bi], func=AF.Exp, scale=-1.0), reads=[("co", bi)], writes=["cog"])
                        op("pool", "tensor_scalar", dict(out=c_og, in0=c_og, scalar1=1.0, scalar2=None, op0=ALU.add), reads=["cog"], writes=["cog"])
                        op("dve", "reciprocal", dict(out=c_og, in_=c_og), reads=["cog"], writes=["cog"])
                        for g4 in range(2):
                            gsl = slice(g4 * 512, (g4 + 1) * 512)
                            op("pe", "matmul", dict(out=ps[4 + g4][:], lhsT=onesH[:], rhs=c_sq[:, gsl], start=True, stop=True), reads=["onesH", "csq"], writes=[("ps", 4 + g4)])
                            op("act", "activation", dict(out=c_rs[:, gsl], in_=ps[4 + g4][:], func=AF.Ln, bias=EPS), reads=[("ps", 4 + g4)], writes=[("crs", g4)])
                            op("act", "activation", dict(out=c_rs[:, gsl], in_=c_rs[:, gsl], func=AF.Exp, scale=-0.5), reads=[("crs", g4)], writes=[("crs", g4)])
                            op("dve", "tensor_tensor", dict(out=c_hs[:, gsl], in0=c_hs[:, gsl], in1=c_rs[:, gsl], op=ALU.mult), reads=[("chs", g4), ("crs", g4)], writes=[("chs", g4)])
                        ho3 = v3(c_ho[oi], 8); hs3 = v3(c_hs, 8); og3 = v3(c_og, 8)
                        for hd in range(8):
                            e = "dve"
                            op(e, "scalar_tensor_tensor", dict(out=ho3[:, hd, :], in0=hs3[:, hd, :], scalar=gall[:, ghc + hd:ghc + hd + 1], in1=og3[:, hd, :], op0=ALU.mult, op1=ALU.mult),
                               reads=[("chs", hd // 4), "cog", "gall"], writes=[("cho", oi)])
                        op("sp", "dma_start", dict(out=hTv[:, :, csl], in_=ho3), reads=[("cho", oi)], writes=[("hT_s", t)], dma=True)

            def set_state(kind, src=None):
                cb_cur = cbn[0] % 2
                if kind == "zero":
                    op("dve", "memset", dict(ap=c_S, constant=0.0), writes=["cS"])
                    op("dve", "memset", dict(ap=c_Cb[cb_cur], constant=0.0), writes=[("cCb", cb_cur)])
                elif kind == "cm":
                    op("dve", "tensor_scalar", dict(out=c_S, in0=c_S, scalar1=cm[:, 0:1], scalar2=None, op0=ALU.mult), reads=["cS", "cm"], writes=["cS"])
                    op("dve", "tensor_scalar", dict(out=c_Cb[cb_cur], in0=c_Cb[cb_cur], scalar1=cm[:, 0:1], scalar2=None, op0=ALU.mult), reads=[("cCb", cb_cur), "cm"], writes=[("cCb", cb_cur)])
                else:
                    op("dve", "tensor_copy", dict(out=c_S, in_=src), reads=["cT"], writes=["cS"])
                    op("act", "activation", dict(out=c_Cb[cb_cur], in_=src, func=AF.Copy), reads=["cT"], writes=[("cCb", cb_cur)])

            nseg = SEG // CH
            if XCH:
                for dirn in (0, 1):
                    op("dve", "memset", dict(ap=c_S, constant=0.0), writes=["cS"])
                    op("dve", "memset", dict(ap=c_gt, constant=0.0), writes=["cgt"])
                    order = list(range(nseg, nchunks)) if dirn == 0 else list(range(nchunks - 1, nseg - 1, -1))
                    for n_i, ck in enumerate(order):
                        for _ in chunk_body(dirn, n_i, ck, "state"):
                            pass
                    op("dve", "tensor_copy", dict(out=c_xst[:, dirn * 1024:(dirn + 1) * 1024], in_=c_S), reads=["cS"], writes=["cxst"])
                    op("act", "activation", dict(out=c_xst[:, 2048 + dirn * 8:2048 + (dirn + 1) * 8], in_=c_gt, func=AF.Exp), reads=["cgt"], writes=["cxst"])
                xin, xout = xch_bufs[("ml", j)]
                op("sp", "dma_start", dict(out=xin, in_=c_xst), reads=["cxst"], writes=[("xin", "ml", j)], dma=True)
                op("pool", "collective_compute", dict(kind="AllGather", op=ALU.bypass, replica_groups=[list(range(8))], ins=[xin.opt()], outs=[xout.opt()]),
                   reads=[("xin", "ml", j)], writes=[("xout", "ml", j)], dma=True, cc=True)
                op("dve", "memset", dict(ap=c_Tf, constant=0.0), writes=["cT"])
                op("dve", "memset", dict(ap=c_Tb, constant=0.0), writes=["cT"])
                nl = [0]
                for dirn in (0, 1):
                    T = c_Tf if dirn == 0 else c_Tb
                    T3 = v3(T, 4)
                    ranks = list(range(8)) if dirn == 0 else list(range(7, -1, -1))
                    for r in ranks:
                        xb = c_xr[nl[0] % 2]; xk = ("cxr", nl[0] % 2)
                        nl[0] += 1
                        op("sp", "dma_start", dict(out=xb, in_=xout[r * 128:(r + 1) * 128, :]), reads=[("xout", "ml", j)], writes=[xk], dma=True)
                        mcol = msk[:, dirn * 16 + r:dirn * 16 + r + 1]
                        mcol1 = msk[:, dirn * 16 + 8 + r:dirn * 16 + 8 + r + 1]
                        egx = xb[:, 2048 + dirn * 8:2048 + (dirn + 1) * 8]
                        for par in range(2):
                            prt = slice(par * 64, par * 64 + 64)
                            op("dve", "tensor_scalar", dict(out=c_A[prt, 0:4], in0=egx[prt, par::2], scalar1=mcol[prt, :], scalar2=mcol1[prt, :], op0=ALU.mult, op1=ALU.add), reads=[xk, "msk"], writes=["cA"])
                        op("dve", "tensor_scalar", dict(out=c_U, in0=xb[:, dirn * 1024:(dirn + 1) * 1024], scalar1=mcol, scalar2=None, op0=ALU.mult), reads=[xk, "msk"], writes=["cU"])
                        for pr in range(4):
                            op("dve", "scalar_tensor_tensor", dict(out=T3[:, pr, :], in0=T3[:, pr, :], scalar=c_A[:, pr:pr + 1], in1=v3(c_U, 4)[:, pr, :], op0=ALU.mult, op1=ALU.add), reads=["cA", "cU", "cT"], writes=["cT"])
            def finish(pend):
                g, dirn, n_i, ck = pend
                a = ck * CH
                if n_i == 0:
                    if XCH and dirn == 1:
                        set_state("copy", c_Tb)
                    else:
                        set_state("zero")
                else:
                    seg_b = (a % SEG == 0) if dirn == 0 else ((a + CH) % SEG == 0)
                    if seg_b:
                        if not XCH:
                            set_state("cm")
                        elif dirn == 0:
                            set_state("copy", c_Tf)
                        else:
                            set_state("zero")
                return g

            NF = 3
            gi = 0
            for dirn in (0, 1):
                order = list(range(nchunks)) if dirn == 0 else list(range(nchunks - 1, -1, -1))
                pend = None
                for n_i, ck in enumerate(order):
                    g = chunk_body(dirn, n_i, ck, "full", gi)
                    gi += 1
                    gb = finish(pend) if pend is not None else None
                    for st_ in range(NF):
                        next(g, None)
                        if gb is not None:
                            next(gb, None)
                    if gb is not None:
                        for _ in gb:
                            pass
                    pend = (g, dirn, n_i, ck)
                gb = finish(pend)
                for _ in gb:
                    pass
            P.barrier()

        def conv_halo_exchange(j):
            xin, xout = xch_bufs[("cv", j)]
            t_f = SEG // NT
            op("sp", "dma_start", dict(out=cx[:, 0:8].unsqueeze(2), in_=zsv[:, :, 1 + SEG:2 + SEG], allow_slow_non_contiguous=True), reads=[("zs", t_f)], writes=["cx"], dma=True)
            op("sp", "dma_start", dict(out=cx[:, 8:16].unsqueeze(2), in_=zsv[:, :, NTOK:NTOK + 1], allow_slow_non_contiguous=True), reads=[("zs", ntiles - 1)], writes=["cx"], dma=True)
            op("sp", "dma_start", dict(out=xin, in_=cx[:]), reads=["cx"], writes=[("xin", "cv", j)], dma=True)
            op("pool", "collective_compute", dict(kind="AllGather", op=ALU.bypass, replica_groups=[list(range(8))], ins=[xin.opt()], outs=[xout.opt()]),
               reads=[("xin", "cv", j)], writes=[("xout", "cv", j)], dma=True, cc=True)
            op("sp", "dma_start", dict(out=hx[:], in_=xout.rearrange("(r p) w -> p r w", p=128)), reads=[("xout", "cv", j)], writes=["hx"], dma=True)
            op("dve", "tensor_scalar", dict(out=hl[:], in0=hx[:, 0, 8:16], scalar1=msk[:, 32:33], scalar2=None, op0=ALU.mult), reads=["hx", "msk"], writes=["hl"])
            op("dve", "tensor_scalar", dict(out=hr[:], in0=hx[:, 0, 0:8], scalar1=msk[:, 40:41], scalar2=None, op0=ALU.mult), reads=["hx", "msk"], writes=["hr"])
            for r in range(1, 8):
                op("dve", "scalar_tensor_tensor", dict(out=hl[:], in0=hx[:, r, 8:16], scalar=msk[:, 32 + r:33 + r], in1=hl[:], op0=ALU.mult, op1=ALU.add), reads=["hx", "msk", "hl"], writes=["hl"])
                op("dve", "scalar_tensor_tensor", dict(out=hr[:], in0=hx[:, r, 0:8], scalar=msk[:, 40 + r:41 + r], in1=hr[:], op0=ALU.mult, op1=ALU.add), reads=["hx", "msk", "hr"], writes=["hr"])

        def ffn1(l):
            ffn(w1g[l], w1u[l], w1d[l], GO["ffn1"] + l * 8)

        def ffn2(l):
            ffn(w2g[l], w2u[l], w2d[l], GO["ffn2"] + l * 8)

        segs = []
        l = 0
        cur_seg = [("loadx", "in")]
        while l < DEPTH:
            j = l // 2
            cur_seg.append(("ffn1", l))
            if l % 2 == 0:
                cur_seg += [("norm_mix", l), ("spill",), ("prefetch",), ("conv_in", j)]
                segs.append(cur_seg)
                cur_seg = [("loadx", "xs"), ("conv_mix", j), ("ffn2", l)]
            else:
                cur_seg += [("norm_mix", l), ("spill",), ("prefetch",), ("mlstm_in", j)]
                segs.append(cur_seg)
                segs.append([("core", j)])
                cur_seg = [("loadx", "xs"), ("mlstm_out", j), ("ffn2", l)]
            l += 1
        cur_seg += [("final",)]
        segs.append(cur_seg)

        prefetched = set()
        for si, seg in enumerate(segs):
            if debug_stop is not None and si >= debug_stop:
                break
            if seg[0][0] == "core":
                mlstm_core(seg[0][1])
                continue
            for t in range(ntiles + 1):
                if t == ntiles:
                    if XCH:
                        for step in seg:
                            if step[0] == "conv_in":
                                conv_halo_exchange(step[1])
                    break
                tsl = slice(t * NT, (t + 1) * NT)
                for step in seg:
                    k = step[0]
                    if k == "loadx":
                        if (si, t) not in prefetched:
                            load_x(xTv if step[1] == "in" else xsv, t)
                    elif k == "prefetch":
                        if t + 1 < ntiles:
                            load_x(xTv if seg[0][1] == "in" else xsv, t + 1)
                            prefetched.add((si, t + 1))
                    elif k == "ffn1":
                        ffn1(step[1])
                    elif k == "ffn2":
                        ffn2(step[1])
                    elif k == "norm_mix":
                        rmsnorm(GO["mix"] + step[1] * 8)
                    elif k == "conv_in":
                        conv_in(step[1], t)
                    elif k == "conv_mix":
                        conv_mix(step[1], t)
                    elif k == "mlstm_in":
                        mlstm_in(step[1], t)
                    elif k == "mlstm_out":
                        for c in range(KC):
                            op("sp", "dma_start", dict(out=xn[:, c, :], in_=hTv[:, c, tsl]), reads=[("hT_s", t)], writes=[("xn", c, 0), ("xn", c, 1)], dma=True)
                        out_proj(wmo[step[1]], 1.0)
                    elif k == "spill":
                        store_x(xsv, t, "xs")
                    elif k == "final":
                        rmsnorm_final = True
                        sq = h
                        for c in range(KC):
                            op("act", "activation", dict(out=sq[:, c, :], in_=x[:, c, :], func=AF.Square), reads=[("x", c, 0), ("x", c, 1)], writes=[("h", c, 0), ("h", c, 1)])
                        for th in range(2):
                            sl = slice(th * 512, (th + 1) * 512)
                            for c in range(KC):
                                op("pe", "matmul", dict(out=ps[6 + th][:], lhsT=onesD[:], rhs=sq[:, c, sl], start=(c == 0), stop=(c == KC - 1)), reads=[("h", c, th), "onesD"], writes=[("ps", 6 + th)])
                            op("act", "activation", dict(out=rstd[:, sl], in_=ps[6 + th][:], func=AF.Ln, bias=EPS), reads=[("ps", 6 + th)], writes=[("rstd", th)])
                            op("act", "activation", dict(out=rstd[:, sl], in_=rstd[:, sl], func=AF.Exp, scale=-0.5), reads=[("rstd", th)], writes=[("rstd", th)])
                        gf = GO["final"]
                        for c in range(KC):
                            for th in range(2):
                                sl = slice(th * 512, (th + 1) * 512)
                                op("dve", "scalar_tensor_tensor", dict(out=x[:, c, sl], in0=x[:, c, sl], scalar=gall[:, gf + c:gf + c + 1], in1=rstd[:, sl], op0=ALU.mult, op1=ALU.mult),
                                   reads=[("x", c, th), ("rstd", th), "gall"], writes=[("x", c, th)])
                        store_x(yTv, t, "yT")
        if debug_stop is not None:
            pass
        P.emit(st)
    return nc


def host_inputs(inp, DEPTH, x_tok, cm_val):
    n_conv = (DEPTH + 1) // 2
    n_ml = DEPTH // 2
    GO = gall_layout(DEPTH)
    f = lambda a: np.ascontiguousarray(np.asarray(a, dtype=np.float32))
    gall = np.zeros((128, GO["_n"]), np.float32)
    colT = lambda v: v.reshape(-1, 8, 128).transpose(2, 0, 1).reshape(128, -1)
    gall[:, GO["ffn1"]:GO["ffn1"] + DEPTH * 8] = colT(f(inp["g_ffn1"])[:DEPTH])
    gall[:, GO["mix"]:GO["mix"] + DEPTH * 8] = colT(f(inp["g_mix"])[:DEPTH])
    gall[:, GO["ffn2"]:GO["ffn2"] + DEPTH * 8] = colT(f(inp["g_ffn2"])[:DEPTH])
    gall[:, GO["final"]:GO["final"] + 8] = colT(f(inp["g_final"])[None])
    if n_ml:
        gall[:, GO["head"]:GO["head"] + n_ml * 8] = colT(f(inp["g_mlstm_head"])[:n_ml])
    dw = f(inp["w_conv_dw"])[:n_conv]
    gall[:, GO["dw"]:GO["dw"] + n_conv * 24] = colT(dw.reshape(n_conv * 3, 1024))
    perm = np.concatenate([np.arange(0, 8), np.arange(16, 24), np.arange(8, 16), np.arange(24, 32)])
    m = {"xT": np.ascontiguousarray(x_tok.T), "gall": gall}
    for k_ in ("w_ffn1_gate", "w_ffn1_up", "w_ffn1_down", "w_ffn2_gate", "w_ffn2_up", "w_ffn2_down"):
        m[k_] = f(inp[k_])[:DEPTH]
    m["w_conv_in"] = f(inp["w_conv_in"])[:n_conv]
    m["w_conv_out"] = f(inp["w_conv_out"])[:n_conv]
    nm = max(n_ml, 1)
    if n_ml == 0:
        inp = dict(inp)
        inp["w_mlstm_in"] = np.zeros((1, 1024, 3072), np.float32); inp["w_mlstm_out"] = np.zeros((1, 1024, 1024), np.float32)
        inp["w_mlstm_gate"] = np.zeros((1, 1024, 32), np.float32); inp["b_mlstm_gate"] = np.zeros((1, 32), np.float32)
    wmi = f(inp["w_mlstm_in"])[:nm]
    m["w_mlstm_in"] = np.ascontiguousarray(np.concatenate([wmi[:, :, 0:1024], wmi[:, :, 1024:3072]], axis=2))
    m["w_mlstm_out"] = f(inp["w_mlstm_out"])[:nm]
    m["w_mlstm_gate"] = np.ascontiguousarray(f(inp["w_mlstm_gate"])[:nm][:, :, perm])
    bg = f(inp["b_mlstm_gate"])[:nm][:, perm].reshape(1, nm * 32)
    m["b_mlstm_gate"] = np.ascontiguousarray(np.broadcast_to(bg, (128, nm * 32)))
    m["cm"] = np.full((128, 1), cm_val, np.float32)
    jj = np.arange(128)[:, None]; tt = np.arange(128)[None, :]
    triLE = (jj <= tt).astype(np.float32); triGE = (jj >= tt).astype(np.float32)
    negFW = np.where(jj <= tt, 0.0, NEG).astype(np.float32); negBW = np.where(jj >= tt, 0.0, NEG).astype(np.float32)
    m["cst"] = np.ascontiguousarray(np.concatenate([triLE, triGE, negFW, negBW, np.eye(128, dtype=np.float32)], axis=1))
    return m


def make_masks(c, n=8, grp=4):
    m = np.zeros((48,), np.float32)
    g = c // grp
    for r in range(n):
        same = (r // grp) == g
        mf = 1.0 if (same and r < c) else 0.0
        mb = 1.0 if (same and r > c) else 0.0
        m[r] = mf; m[8 + r] = 1.0 - mf
        m[16 + r] = mb; m[24 + r] = 1.0 - mb
        m[32 + r] = 1.0 if (same and r == c - 1) else 0.0
        m[40 + r] = 1.0 if (same and r == c + 1) else 0.0
    return np.ascontiguousarray(np.broadcast_to(m[None, :], (128, 48)))


NTOK_CORE = 8192
SEGLEN = 4096
NLAYER = 4
_KEYS = ["g_ffn1", "w_ffn1_gate", "w_ffn1_up", "w_ffn1_down", "g_mix", "w_conv_in", "w_conv_dw", "w_conv_out",
         "w_mlstm_in", "w_mlstm_gate", "b_mlstm_gate", "g_mlstm_head", "w_mlstm_out", "g_ffn2", "w_ffn2_gate",
         "w_ffn2_up", "w_ffn2_down", "g_final"]


def kernel(x_prompt, x_sample, **w):
    x_prompt = np.asarray(x_prompt, dtype=np.float32)
    x_sample = np.asarray(x_sample, dtype=np.float32)
    inp = {k: np.asarray(w[k]) for k in _KEYS}
    nc = build(NTOK_CORE, SEGLEN, NLAYER, XCH=True)
    shared = host_inputs(inp, NLAYER, x_prompt[0], 0.0)
    in_maps = []
    for c in range(8):
        g, q = c // 4, c % 4
        m = dict(shared)
        m["xT"] = np.ascontiguousarray(np.concatenate([x_prompt[c].T, x_sample[g, q * SEGLEN:(q + 1) * SEGLEN].T], axis=1))
        m["msk"] = make_masks(c)
        in_maps.append(m)
    res = run_bass_kernel_spmd(nc, in_maps, core_ids=list(range(8)))
    y_prompt = np.empty((8, SEGLEN, D), np.float32)
    y_sample = np.empty((2, 4 * SEGLEN, D), np.float32)
    for c in range(8):
        yT = res.results[c]["yT"]
        y_prompt[c] = yT[:, 0:SEGLEN].T
        y_sample[c // 4, (c % 4) * SEGLEN:(c % 4 + 1) * SEGLEN] = yT[:, SEGLEN:].T
    return (y_prompt, y_sample)
```
